# Optimizing a Trainium2 kernel written in Bass

```python
import math
import jax, jax.numpy as jnp
from jax import lax
import numpy as np

D_MODEL = 1024
BATCH = 32
SEQ = 2048
DEPTH = 1

D_MIX = D_MODEL
NSA_HEADS = 8
NSA_KV_GROUPS = 2
NSA_HEAD_DIM = 64
NSA_HPG = NSA_HEADS // NSA_KV_GROUPS
CMP_BLOCK = 32
CMP_STRIDE = 16
CMP_HIDDEN = 256
SLC_BLOCK = 64
SLC_TOPK = 8
WINDOW = 512
FORCE_SCORE = 1e4
GLA_HEADS = 4
GLA_DK = 64
GLA_DV = 128
GLA_GATE_RANK = 16
GLA_TAU = 16.0
GLA_CHUNK = 64
ROPE_THETA = 500000.0
ROT_DIM = NSA_HEAD_DIM // 4
D_FF = 2816
CONV_W = 3
EPS = 1e-6
Q_BLOCK = 64
NEG = -1e30

NSA_Q_W = NSA_HEADS * NSA_HEAD_DIM
NSA_KV_W = NSA_KV_GROUPS * NSA_HEAD_DIM
GLA_QK_W = GLA_HEADS * GLA_DK
GLA_V_W = GLA_HEADS * GLA_DV
IN_SPLITS = [NSA_Q_W, 6 * NSA_KV_W, 3 * NSA_HEADS, GLA_QK_W, GLA_QK_W, GLA_V_W, GLA_GATE_RANK, GLA_V_W]
D_IN = sum(IN_SPLITS)

kernel_name = 'hymba_nsa_gla_convffn_block'


def rms_norm(x, g):
    xf = x.astype(jnp.float32)
    y = xf * lax.rsqrt(jnp.mean(xf * xf, axis=-1, keepdims=True) + EPS)
    return (y * g.astype(jnp.float32)).astype(x.dtype)


def partial_rope(x, pos):
    half = ROT_DIM // 2
    inv = ROPE_THETA ** (-jnp.arange(half, dtype=jnp.float32) / half)
    ang = pos.astype(jnp.float32)[:, None] * inv[None, :]
    cos = jnp.cos(ang)[:, None, :]
    sin = jnp.sin(ang)[:, None, :]
    xr = x[..., :ROT_DIM].astype(jnp.float32)
    x1, x2 = xr[..., :half], xr[..., half:]
    rot = jnp.concatenate([x1 * cos - x2 * sin, x2 * cos + x1 * sin], axis=-1).astype(x.dtype)
    return jnp.concatenate([rot, x[..., ROT_DIM:]], axis=-1)


def masked_softmax(s, mask):
    s = jnp.where(mask, s.astype(jnp.float32), NEG)
    return jax.nn.softmax(s, axis=-1) * mask


def nsa_mixer(q, kv, gate_logits, q_gain, k_gains, pe_k, pe_v, w1_k, w2_k, w1_v, w2_v):
    B, S = q.shape[0], q.shape[1]
    dt = q.dtype
    G, HPG, D = NSA_KV_GROUPS, NSA_HPG, NSA_HEAD_DIM
    scale = D ** -0.5
    pos = jnp.arange(S)
    kc, vc, ks, vs, kw, vw = [c.reshape(B, S, G, D) for c in jnp.split(kv, 6, axis=-1)]
    q = partial_rope(rms_norm(q.reshape(B, S, NSA_HEADS, D), q_gain), pos)
    q = q.reshape(B, S, G, HPG, D) * scale

    n_cmp = (S - CMP_BLOCK) // CMP_STRIDE + 1
    blk = np.arange(n_cmp)[:, None] * CMP_STRIDE + np.arange(CMP_BLOCK)[None, :]

    def compress(t, pe, w1, w2):
        tb = t[:, blk] + pe[:, None, :]
        tb = jnp.moveaxis(tb, 3, 2).reshape(B, n_cmp, G, CMP_BLOCK * D)
        return jax.nn.gelu(tb @ w1) @ w2

    cmp_end = jnp.arange(n_cmp) * CMP_STRIDE + CMP_BLOCK - 1
    k_cmp = partial_rope(rms_norm(compress(kc, pe_k, w1_k, w2_k), k_gains[0]), cmp_end)
    v_cmp = compress(vc, pe_v, w1_v, w2_v)
    s_cmp = jnp.einsum('bsghd,bngd->bghsn', q, k_cmp)
    p_cmp = masked_softmax(s_cmp, cmp_end[None, :] <= pos[:, None])
    o_cmp = jnp.einsum('bghsn,bngd->bsghd', p_cmp.astype(dt), v_cmp).reshape(B, S, NSA_HEADS, D)

    n_slc = S // SLC_BLOCK
    top_k = min(SLC_TOPK, n_slc)
    ci = np.arange(n_cmp)[:, None]
    sj = np.arange(n_slc)[None, :]
    overlap = ((ci * CMP_STRIDE < (sj + 1) * SLC_BLOCK) &
               (ci * CMP_STRIDE + CMP_BLOCK > sj * SLC_BLOCK)).astype(np.float32)
    p_slc = jnp.einsum('bghsn,nj->bgsj', p_cmp, jnp.asarray(overlap))
    j = jnp.arange(n_slc)[None, :]
    cur = (pos // SLC_BLOCK)[:, None]
    forced = (j == 0) | (j == cur) | (j == cur - 1)
    causal_blk = j * SLC_BLOCK <= pos[:, None]
    score = jnp.where(causal_blk, jnp.where(forced, FORCE_SCORE, p_slc), -jnp.inf)
    _, sel_idx = lax.top_k(score, top_k)

    ks = partial_rope(rms_norm(ks, k_gains[1]), pos)
    kw = partial_rope(rms_norm(kw, k_gains[2]), pos)
    ks_blk = jnp.moveaxis(ks.reshape(B, n_slc, SLC_BLOCK, G, D), 3, 1)
    vs_blk = jnp.moveaxis(vs.reshape(B, n_slc, SLC_BLOCK, G, D), 3, 1)
    pad = ((0, 0), (WINDOW, 0), (0, 0), (0, 0))
    kw_pad = jnp.pad(kw, pad)
    vw_pad = jnp.pad(vw, pad)
    n_qb = S // Q_BLOCK
    q_blocks = jnp.moveaxis(q.reshape(B, n_qb, Q_BLOCK, G, HPG, D), 1, 0)
    idx_blocks = jnp.moveaxis(sel_idx.reshape(B, G, n_qb, Q_BLOCK, top_k), 2, 0)
    bi = jnp.arange(B)[:, None, None, None]
    gi = jnp.arange(G)[None, :, None, None]

    def block_fn(args):
        i, qb, ib = args
        start = i * Q_BLOCK
        tq = start + jnp.arange(Q_BLOCK)
        k_sel = ks_blk[bi, gi, ib].reshape(B, G, Q_BLOCK, top_k * SLC_BLOCK, D)
        v_sel = vs_blk[bi, gi, ib].reshape(B, G, Q_BLOCK, top_k * SLC_BLOCK, D)
        kpos = (ib[..., None] * SLC_BLOCK + jnp.arange(SLC_BLOCK)).reshape(B, G, Q_BLOCK, top_k * SLC_BLOCK)
        m_sel = kpos <= tq[None, None, :, None]
        s = jnp.einsum('bqghd,bgqkd->bghqk', qb, k_sel)
        p = masked_softmax(s, m_sel[:, :, None])
        o_sel = jnp.einsum('bghqk,bgqkd->bqghd', p.astype(dt), v_sel)
        k_win = lax.dynamic_slice_in_dim(kw_pad, start, WINDOW + Q_BLOCK, axis=1)
        v_win = lax.dynamic_slice_in_dim(vw_pad, start, WINDOW + Q_BLOCK, axis=1)
        wpos = start - WINDOW + jnp.arange(WINDOW + Q_BLOCK)
        m_win = ((wpos[None, :] <= tq[:, None]) & (wpos[None, :] > tq[:, None] - WINDOW)
                 & (wpos[None, :] >= 0))
        s = jnp.einsum('bqghd,bkgd->bghqk', qb, k_win)
        p = masked_softmax(s, m_win)
        o_win = jnp.einsum('bghqk,bkgd->bqghd', p.astype(dt), v_win)
        return o_sel, o_win

    o_sel, o_win = lax.map(block_fn, (jnp.arange(n_qb), q_blocks, idx_blocks))
    o_sel = jnp.moveaxis(o_sel, 0, 1).reshape(B, S, NSA_HEADS, D)
    o_win = jnp.moveaxis(o_win, 0, 1).reshape(B, S, NSA_HEADS, D)

    g = jax.nn.sigmoid(gate_logits.reshape(B, S, NSA_HEADS, 3).astype(jnp.float32)).astype(dt)
    o = g[..., 0:1] * o_cmp + g[..., 1:2] * o_sel + g[..., 2:3] * o_win
    return o.reshape(B, S, NSA_Q_W)


def gla_mixer(q, k, v, g_lr, g_out, w_gate2, b_gate, out_gain):
    B, S = q.shape[0], q.shape[1]
    dt = q.dtype
    H, L = GLA_HEADS, GLA_CHUNK
    C = S // L

    def heads(t, d):
        return t.astype(jnp.float32).reshape(B, C, L, H, d).transpose(0, 3, 1, 2, 4)

    log_a = jax.nn.log_sigmoid((g_lr @ w_gate2 + b_gate).astype(jnp.float32)) / GLA_TAU
    qh = heads(q, GLA_DK) * (GLA_DK ** -0.5)
    kh = heads(k, GLA_DK)
    vh = heads(v, GLA_DV)
    b = jnp.cumsum(heads(log_a, GLA_DK), axis=3)
    b_last = b[:, :, :, -1:, :]
    q_t = qh * jnp.exp(b)
    k_t = kh * jnp.exp(-b)
    tril = jnp.tril(jnp.ones((L, L), dtype=bool))
    A = jnp.where(tril, jnp.einsum('bhcid,bhcjd->bhcij', q_t, k_t), 0.0)
    o_intra = jnp.einsum('bhcij,bhcjv->bhciv', A, vh)
    dS = jnp.einsum('bhcjd,bhcjv->bhcdv', kh * jnp.exp(b_last - b), vh)
    decay = jnp.exp(b_last[:, :, :, 0, :])

    def step(state, xs):
        dec, ds = xs
        return dec[..., None] * state + ds, state

    init = jnp.zeros((B, H, GLA_DK, GLA_DV), jnp.float32)
    _, s_prev = lax.scan(step, init, (jnp.moveaxis(decay, 2, 0), jnp.moveaxis(dS, 2, 0)))
    s_prev = jnp.moveaxis(s_prev, 0, 2)
    o = o_intra + jnp.einsum('bhcid,bhcdv->bhciv', q_t, s_prev)
    o = o.transpose(0, 2, 3, 1, 4).reshape(B, S, H, GLA_DV)
    o = rms_norm(o, out_gain) * jax.nn.silu(g_out.astype(jnp.float32).reshape(B, S, H, GLA_DV))
    return o.reshape(B, S, GLA_V_W).astype(dt)


def causal_dwconv(u, w, b):
    C = u.shape[-1]
    y = lax.conv_general_dilated(u, w[:, None, :].astype(u.dtype), window_strides=(1,),
                                 padding=[(CONV_W - 1, 0)],
                                 dimension_numbers=('NWC', 'WIO', 'NWC'),
                                 feature_group_count=C)
    return y + b


def setup_inputs(seed: int = 0) -> dict:
    key = jax.random.key(seed)
    ks = jax.random.split(key, 24)
    L = DEPTH

    def nrm(k, shape, scale):
        return jax.random.normal(k, shape, jnp.float32) * scale

    def gain(k, shape):
        return 1.0 + 0.05 * jax.random.normal(k, shape, jnp.float32)

    cmp_in = CMP_BLOCK * NSA_HEAD_DIM
    return {
        'x': nrm(ks[0], (BATCH, SEQ, D_MODEL), 1.0),
        'attn_norm': gain(ks[1], (L, D_MODEL)),
        'w_in': nrm(ks[2], (L, D_MODEL, D_IN), D_MODEL ** -0.5),
        'nsa_q_norm': gain(ks[3], (L, NSA_HEAD_DIM)),
        'nsa_k_norm': gain(ks[4], (L, 3, NSA_HEAD_DIM)),
        'cmp_pos_k': nrm(ks[5], (L, CMP_BLOCK, NSA_HEAD_DIM), 0.1),
        'cmp_pos_v': nrm(ks[6], (L, CMP_BLOCK, NSA_HEAD_DIM), 0.1),
        'cmp_w1_k': nrm(ks[7], (L, cmp_in, CMP_HIDDEN), cmp_in ** -0.5),
        'cmp_w2_k': nrm(ks[8], (L, CMP_HIDDEN, NSA_HEAD_DIM), CMP_HIDDEN ** -0.5),
        'cmp_w1_v': nrm(ks[9], (L, cmp_in, CMP_HIDDEN), cmp_in ** -0.5),
        'cmp_w2_v': nrm(ks[10], (L, CMP_HIDDEN, NSA_HEAD_DIM), CMP_HIDDEN ** -0.5),
        'gla_w_gate2': nrm(ks[11], (L, GLA_GATE_RANK, GLA_QK_W), GLA_GATE_RANK ** -0.5),
        'gla_b_gate': nrm(ks[12], (L, GLA_QK_W), 0.1),
        'gla_out_norm': gain(ks[13], (L, GLA_DV)),
        'w_out': nrm(ks[14], (L, D_MIX, D_MODEL), D_MIX ** -0.5),
        'ffn_norm': gain(ks[15], (L, D_MODEL)),
        'w_up': nrm(ks[16], (L, D_MODEL, 2 * D_FF), D_MODEL ** -0.5),
        'conv_w': nrm(ks[17], (L, CONV_W, 2 * D_FF), CONV_W ** -0.5),
        'conv_b': nrm(ks[18], (L, 2 * D_FF), 0.02),
        'w_down': nrm(ks[19], (L, D_FF, D_MODEL), D_FF ** -0.5),
    }


def reference(x, attn_norm, w_in, nsa_q_norm, nsa_k_norm, cmp_pos_k, cmp_pos_v, cmp_w1_k, cmp_w2_k,
              cmp_w1_v, cmp_w2_v, gla_w_gate2, gla_b_gate, gla_out_norm, w_out, ffn_norm, w_up,
              conv_w, conv_b, w_down):
    offs = [int(o) for o in np.cumsum(IN_SPLITS)[:-1]]
    h = x
    for l in range(DEPTH):
        z = rms_norm(h, attn_norm[l]) @ w_in[l]
        nq, nkv, ngate, gq, gk, gv, glr, gout = jnp.split(z, offs, axis=-1)
        o_nsa = nsa_mixer(nq, nkv, ngate, nsa_q_norm[l], nsa_k_norm[l], cmp_pos_k[l], cmp_pos_v[l],
                          cmp_w1_k[l], cmp_w2_k[l], cmp_w1_v[l], cmp_w2_v[l])
        o_gla = gla_mixer(gq, gk, gv, glr, gout, gla_w_gate2[l], gla_b_gate[l], gla_out_norm[l])
        h = h + jnp.concatenate([o_nsa, o_gla], axis=-1) @ w_out[l]
        u = causal_dwconv(rms_norm(h, ffn_norm[l]) @ w_up[l], conv_w[l], conv_b[l])
        gate, up = jnp.split(u, 2, axis=-1)
        h = h + (jax.nn.silu(gate) * up) @ w_down[l]
    return h
```

```python
import contextlib
import numpy as np
import concourse.bass as bass
import concourse.mybir as mybir
from concourse.bass_utils import run_bass_kernel_spmd

F32 = mybir.dt.float32
BF16 = mybir.dt.bfloat16
AF = mybir.ActivationFunctionType
ALU = mybir.AluOpType
AX = mybir.AxisListType

S = 2048
DM = 1024
NT = 16
D_IN = 2856
DFF = 2816
NFC = 22
EPS = 1e-6
NCORES = 8
G1_MAXGI = [6]
LN8 = float(np.log(0.125))

ENG_NAMES = ("pe", "act", "dve", "pool", "sp")
PSUM_KEYS = {("HB", 0), ("HB", 1), ("HB", 2), ("HB", 3), ("SC", 0), ("SC", 1), "MK", "TR"}


class Op:
    __slots__ = ("eng", "fn", "idx", "dma", "deps", "signal", "token", "ndma", "dsem")

    def __init__(self, eng, fn, idx, dma):
        self.eng = eng
        self.fn = fn
        self.idx = idx
        self.dma = dma
        self.deps = []
        self.signal = False
        self.token = None
        self.ndma = 1
        self.dsem = None


class Prog:
    def __init__(self, n_dma_sems=8):
        self.ops = {e: [] for e in ENG_NAMES}
        self.last_write = {}
        self.readers = {}
        self.n_dma_sems = n_dma_sems
        self.dma_rr = {e: 0 for e in ENG_NAMES}
        self.dma_last = {}

    def op(self, eng, fn, reads=(), writes=(), dma=False, ndma=1):
        o = Op(eng, fn, len(self.ops[eng]), dma)
        o.ndma = ndma
        deps = {}
        ex = [k for k in reads if k in PSUM_KEYS]
        if ex:
            reads = [k for k in reads if k not in PSUM_KEYS]
            writes = list(writes) + ex
        for k in reads:
            w = self.last_write.get(k)
            if w is not None:
                deps[id(w)] = w
        for k in writes:
            w = self.last_write.get(k)
            if w is not None:
                deps[id(w)] = w
            for r in self.readers.get(k, ()):
                deps[id(r)] = r
        if dma:
            si = self.dma_rr[eng] % self.n_dma_sems
            self.dma_rr[eng] += 1
            o.dsem = (eng, si)
            prev = self.dma_last.get(o.dsem)
            if prev is not None:
                deps[id(prev)] = prev
            self.dma_last[o.dsem] = o
        deps.pop(id(o), None)
        o.deps = list(deps.values())
        for d in o.deps:
            d.signal = True
        for k in reads:
            self.readers.setdefault(k, []).append(o)
        for k in writes:
            self.last_write[k] = o
            self.readers[k] = []
        self.ops[eng].append(o)
        return o

    def wait_all(self, eng, ops):
        o = Op(eng, None, len(self.ops[eng]), False)
        o.deps = list(ops)
        for d in o.deps:
            d.signal = True
        self.ops[eng].append(o)
        return o

    def emit(self, nc, stack):
        esem = {e: stack.enter_context(nc.semaphore("s_" + e)) for e in ENG_NAMES}
        dsem = {}
        for e in ENG_NAMES:
            if any(o.dma for o in self.ops[e]):
                for i in range(self.n_dma_sems):
                    dsem[(e, i)] = stack.enter_context(nc.semaphore("d_%s%d" % (e, i)))
        dcount = {}
        for e in ENG_NAMES:
            c = 0
            for o in self.ops[e]:
                if o.fn is None:
                    continue
                if o.dma:
                    dcount[o.dsem] = dcount.get(o.dsem, 0) + 16 * o.ndma
                    o.token = (dsem[o.dsem], dcount[o.dsem])
                elif o.signal:
                    c += 1
                    o.token = (esem[e], c)
        ops = self.ops
        block = stack.enter_context(nc.Block())

        def run(e, engine):
            seen = {}
            for o in ops[e]:
                for d in o.deps:
                    if (not d.dma) and d.eng == e and e == "pe":
                        continue
                    sem, val = d.token
                    key = id(sem)
                    if seen.get(key, 0) >= val:
                        continue
                    seen[key] = val
                    engine.wait_ge(sem, val)
                if o.fn is None:
                    continue
                ins = o.fn(engine)
                if o.dma:
                    lst = ins if isinstance(ins, (list, tuple)) else [ins]
                    assert len(lst) == o.ndma, (len(lst), o.ndma)
                    for i_ in lst:
                        i_.then_inc(o.token[0], 16)
                elif o.signal:
                    ins.then_inc(o.token[0], 1)

        @block.tensor
        def _(engine):
            run("pe", engine)

        @block.scalar
        def _(engine):
            run("act", engine)

        @block.vector
        def _(engine):
            run("dve", engine)

        @block.gpsimd
        def _(engine):
            run("pool", engine)

        @block.sync
        def _(engine):
            run("sp", engine)


def make_consts():
    c = {}
    c["ident"] = np.eye(128, dtype=np.float32)
    k = np.arange(128)[:, None]
    q = np.arange(128)[None, :]
    c["cm"] = (k <= q).astype(np.float32)
    c["bm"] = (k > q).astype(np.float32)
    same = (k // 64) == (q // 64)
    c["tcum"] = (same & (k <= q)).astype(np.float32)
    c["tafter"] = (same & (k > q)).astype(np.float32)
    ci = np.zeros((128, 2), np.float32)
    ci[:64, 0] = 1.0
    ci[64:, 1] = 1.0
    c["chunkind"] = ci
    cc = np.arange(128)[:, None]
    pos = np.arange(S)[None, :]
    n = cc - 1
    c["cmpmask"] = ((cc >= 1) & (16 * n + 31 <= pos)).astype(np.float32)
    sj = np.arange(32)[None, :]
    ov = ((cc >= 1) & (n * 16 < (sj + 1) * 64) & (n * 16 + 32 > sj * 64)).astype(np.float32)
    c["overlap"] = ov
    p_ = np.arange(S)[:, None]
    cur = p_ // 64
    forced = (sj == 0) | (sj == cur) | (sj == cur - 1)
    causal = sj * 64 <= p_
    sb = np.where(causal, np.where(forced, 1e4, 0.0), -1e30).astype(np.float32)
    c["selbias"] = sb
    em = np.zeros((32, S), np.float32)
    em[np.arange(S) // 64, np.arange(S)] = 1.0
    c["emat"] = em
    half = 8
    inv = (500000.0 ** (-np.arange(half, dtype=np.float32) / half)).astype(np.float32)
    ang = np.arange(S, dtype=np.float32)[:, None] * inv[None, :]
    c["ropecs"] = np.concatenate([np.cos(ang), np.sin(ang)], axis=1).astype(np.float32)
    endp = (np.arange(128) - 1) * 16 + 31
    endp[0] = 0
    ange = endp.astype(np.float32)[:, None] * inv[None, :]
    c["ropecmp"] = np.concatenate([np.cos(ange), np.sin(ange)], axis=1).astype(np.float32)
    return c


CONST_SHAPES = {
    "ident": [128, 128], "cm": [128, 128], "bm": [128, 128], "tcum": [128, 128], "tafter": [128, 128],
    "chunkind": [128, 2], "cmpmask": [128, S], "overlap": [128, 32], "selbias": [S, 32],
    "emat": [32, S], "ropecs": [S, 16], "ropecmp": [128, 16],
}

WEIGHT_SHAPES = {
    "attn_norm": [1, DM], "w_in": [DM, D_IN], "nsa_q_norm": [1, 64], "nsa_k_norm": [3, 64],
    "cmp_pos_k": [32, 64], "cmp_pos_v": [32, 64], "cmp_w1_k": [2048, 256], "cmp_w2_k": [256, 64],
    "cmp_w1_v": [2048, 256], "cmp_w2_v": [256, 64], "gla_w_gate2": [16, 256], "gla_b_gate": [1, 256],
    "gla_out_norm": [1, 128], "w_out": [DM, DM], "ffn_norm": [1, DM], "w_up": [DM, 2 * DFF],
    "conv_w": [3, 2 * DFF], "conv_b": [1, 2 * DFF], "w_down": [DFF, DM],
}

WIN_PIECES = [
    (0, 0, 512),
    (512, 512, 256),
    (768, 896, 128),
    (896, 1152, 128),
    (1024, 768, 128),
    (1152, 1024, 128),
    (1280, 1280, 24),
    (1304, 2328, 16),
    (1320, 1304, 512),
    (1832, 1816, 512),
    (2344, 2344, 512),
]
WIN_GROUPS = [(0, 512), (512, 512), (1024, 296), (1320, 512), (1832, 512), (2344, 512)]


def bcast_rows(ap, nrows):
    n = ap.shape[-1]
    return bass.AP(ap.tensor, ap.offset, [[0, nrows], [1, n]])


def build_program(nseq, dbg_names=(), stop=99):
    nc = bass.Bass("TRN2", target_bir_lowering=False)
    D = {}
    D["x"] = nc.dram_tensor("x", [nseq * S, DM], F32, kind="ExternalInput").ap()
    for k, shp in WEIGHT_SHAPES.items():
        D[k] = nc.dram_tensor(k, shp, F32, kind="ExternalInput").ap()
    for k, shp in CONST_SHAPES.items():
        D[k] = nc.dram_tensor("c_" + k, shp, F32, kind="ExternalInput").ap()
    out = nc.dram_tensor("out", [nseq * S, DM], F32, kind="ExternalOutput").ap()
    win_bf = nc.dram_tensor("win_bf", [DM, D_IN], BF16, kind="Internal").ap()
    wout_bf = nc.dram_tensor("wout_bf", [DM, DM], BF16, kind="Internal").ap()
    wup_bf = nc.dram_tensor("wup_bf", [DM, 2 * DFF], BF16, kind="Internal").ap()
    wdn_bf = nc.dram_tensor("wdn_bf", [DFF, DM], BF16, kind="Internal").ap()
    w1k_bf = nc.dram_tensor("w1k_bf", [2048, 256], BF16, kind="Internal").ap()
    w1v_bf = nc.dram_tensor("w1v_bf", [2048, 256], BF16, kind="Internal").ap()
    dbg_out = {}

    p = Prog()
    with contextlib.ExitStack() as st:
        def sb(name, shape, dt=F32):
            return st.enter_context(nc.sbuf_tensor(name, shape, dt))

        def ps(name, shape, dt=F32):
            return st.enter_context(nc.psum_tensor(name, shape, dt))

        def dbg(name, tile_ap, shape, reads):
            if name not in dbg_names:
                return
            t = nc.dram_tensor("dbg_" + name, shape, tile_ap.dtype, kind="ExternalOutput").ap()
            dbg_out[name] = t
            p.op("sp", lambda e: e.dma_start(out=t, in_=tile_ap), reads=reads, dma=True)

        HB = ps("HB", [128, 4, 512])
        SC = ps("SC", [128, 2, 512])
        MK = ps("MK", [128, 512])
        TR = ps("TR", [128, 8, 128], BF16)

        ident = sb("ident", [128, 128], BF16)
        cm = sb("cm", [128, 128], BF16)
        bm = sb("bm", [128, 128], BF16)
        tcum_bf = sb("tcum_bf", [128, 128], BF16)
        tcum = sb("tcum", [128, 128])
        tafter = sb("tafter", [128, 128])
        chunkind = sb("chunkind", [128, 2])
        cmpmask = sb("cmpmask", [128, S], BF16)
        selbias = sb("selbias", [128, NT, 32])
        emat = sb("emat", [32, S], BF16)
        ropecs = sb("ropecs", [128, NT, 16])
        ropecmpg = sb("ropecmpg", [32, 4, 16])
        ones_row = sb("ones_row", [1, 128], BF16)
        ident_f = sb("ident_f", [128, 128])
        cst = sb("cst", [44, 4, 128])
        pest = sb("pest", [32, 2, 64], BF16)
        g1b = sb("g1b", [128, DM])
        g2b = sb("g2b", [128, DM])
        gq_b = sb("gq_b", [128, 8, 64])
        gk_b = sb("gk_b", [128, 4, 64])
        gkc_b = sb("gkc_b", [128, 2, 64])
        go_b = sb("go_b", [128, 4, 128])
        wg2 = sb("wg2", [16, 256], BF16)
        bgate = sb("bgate", [1, 256], BF16)
        w2k = sb("w2k", [128, 2, 64], BF16)
        w2v = sb("w2v", [128, 2, 64], BF16)
        peT = sb("peT", [64, 2, 32], BF16)
        c1 = sb("c1", [128, 2, 2])
        convw = sb("convw", [128, 3, 2 * NFC])
        convb = sb("convb", [128, 2 * NFC])
        woutc = [sb("woutc%d" % i, [128, DM], BF16) for i in range(2)]
        dummy = sb("dummy_t", [1, 8])

        xt = [sb("xt%d" % i, [128, DM]) for i in range(2)]
        junk = sb("junk", [128, DM], BF16)
        st4 = sb("st4", [128, 16])
        xn = sb("xn", [128, DM], BF16)
        RA = sb("RA", [128, 6144])
        RB = sb("RB", [128, 6400])
        wing = [RA[:, i * 2048:(i + 1) * 2048].bitcast(BF16).rearrange("p (k c) -> p k c", k=8) for i in range(2)]
        xnT = RA[:, 4096:6144].bitcast(BF16).rearrange("p (a k c) -> p a k c", a=4, k=8)
        actT = RA[:, 0:5632].bitcast(BF16).rearrange("p (f c) -> p f c", f=NFC)
        eB = RB[:, 0:3072].rearrange("p (a b c) -> p a b c", a=4, b=3)
        sgo = RB[:, 3072:5120].rearrange("p (a c) -> p a c", a=4)
        gvb = RB[:, 5120:6144].bitcast(BF16).rearrange("p (a c) -> p a c", a=4)
        ubuf = [[RB[:, (i * 2 + j) * 514:(i * 2 + j + 1) * 514] for j in range(2)] for i in range(2)]
        ybuf = [RB[:, 2056 + j * 512:2056 + (j + 1) * 512] for j in range(2)]
        wupc = [RB[:, 3080 + i * 1024:3080 + (i + 1) * 1024].bitcast(BF16).rearrange("p (k c) -> p k c", k=8) for i in range(2)]
        wdnc = [RB[:, 5128 + i * 512:5128 + (i + 1) * 512].bitcast(BF16) for i in range(2)]
        sq = sb("sq", [128, 512])
        qn = sb("qn", [128, 512])
        ropeA = sb("ropeA", [128, 8, 16])
        ropeB = sb("ropeB", [128, 8, 16])
        qb = sb("qb", [128, 4, 2, 64], BF16)
        qTg = sb("qTg", [128, 4, 4, 128], BF16)
        kb = sb("kb", [128, 4, 64], BF16)
        kvb = sb("kvb", [128, 256], BF16)
        ksT = sb("ksT", [128, S], BF16)
        kwT = sb("kwT", [128, S], BF16)
        vsx = sb("vsx", [128, NT, 2, 65], BF16)
        vwx = sb("vwx", [128, NT, 2, 65], BF16)
        kcTg = sb("kcTg", [128, 544], BF16)
        vcTg = sb("vcTg", [128, 544], BF16)
        sigg = sb("sigg", [128, 4, 24])
        w1c = [sb("w1c%d" % i, [128, 8, 256], BF16) for i in range(2)]
        hTc = sb("hTc", [128, 2, 2, 2, 32], BF16)
        kcn = sb("kcn", [32, 2, 64])
        kcb = sb("kcb", [32, 2, 64], BF16)
        kcmpT = sb("kcmpT", [128, 128], BF16)
        vcmpT = sb("vcmpT", [64, 2, 128], BF16)
        VcX = sb("VcX", [128, 2, 97], BF16)
        glrb = sb("glrb", [128, 16], BF16)
        glrT = sb("glrT", [16, 128], BF16)
        e1 = sb("e1", [128, 256])
        lsp2 = sb("lsp2", [128, 4, 2, 64])
        decayg = sb("decayg", [128, 4, 4, 2])
        qtb2 = sb("qtb2", [128, 4, 2, 64], BF16)
        ktb2 = sb("ktb2", [128, 4, 2, 64], BF16)
        kdec2 = sb("kdec2", [128, 4, 4, 2, 64], BF16)
        qkT = sb("qkT", [128, 4, 8, 128], BF16)
        Sst = sb("Sst", [128, 4, 128])
        S2_bf = sb("S2_bf", [128, 4, 128], BF16)
        ATm = sb("ATm", [128, 4, 128], BF16)
        og = sb("og", [128, 4, 128])
        obf = sb("obf", [128, DM], BF16)
        oT = sb("oT", [128, 8, 128], BF16)
        Pb = [sb("Pb%d" % i, [128, 4, 128], BF16) for i in range(3)]
        score = sb("score", [128, 32])
        top8 = sb("top8", [128, 8])
        selm = sb("selm", [128, 32], BF16)
        selT = sb("selT", [32, 128], BF16)
        rinv = sb("rinv", [128, 3, 4])
        gf = sb("gf", [128, 3, 4])
        oacc = sb("oacc", [128, 4, 64])
        otmp = sb("otmp", [128, 4, 64])
        hres = sb("hres", [128, 4, DM])
        hn = sb("hn", [128, DM], BF16)
        hnT = sb("hnT", [128, 8, 512], BF16)
        halo = sb("halo", [128, 2 * NFC, 2])

        for (dst, src, w) in WIN_PIECES:
            p.op("pool", (lambda dst, src, w: lambda e: e.dma_start(out=win_bf[:, dst:dst + w], in_=D["w_in"][:, src:src + w]))(dst, src, w),
                 writes=[("win_bf", dst)], dma=True)
        for r in range(8):
            p.op("pool", (lambda r: lambda e: e.dma_start(out=wout_bf[r * 128:(r + 1) * 128, :], in_=D["w_out"][r * 128:(r + 1) * 128, :]))(r),
                 writes=[("wout_bf", r)], dma=True)
        for nm, dst_, src_ in (("w1k", w1k_bf, D["cmp_w1_k"]), ("w1v", w1v_bf, D["cmp_w1_v"])):
            for r in range(4):
                p.op("pool", (lambda r, dst_, src_: lambda e: e.dma_start(out=dst_[r * 512:(r + 1) * 512, :], in_=src_[r * 512:(r + 1) * 512, :]))(r, dst_, src_),
                     writes=[(nm, r)], dma=True)

        def cload(eng, dst_ap, src_ap, key):
            p.op(eng, lambda e: e.dma_start(out=dst_ap, in_=src_ap), writes=[key], dma=True)

        cload("pool", ident[:], D["ident"], "ident")
        cload("pool", cm[:], D["cm"], "cm")
        cload("pool", bm[:], D["bm"], "bm")
        cload("pool", tcum_bf[:], D["tcum"], "tcum_bf")
        cload("sp", tcum[:], D["tcum"], "tcum")
        cload("sp", tafter[:], D["tafter"], "tafter")
        cload("sp", chunkind[:], D["chunkind"], "chunkind")
        cload("pool", cmpmask[:], D["cmpmask"], "cmpmask")
        cload("sp", selbias[:], D["selbias"].rearrange("(t p) j -> p t j", p=128), "selbias")
        cload("pool", emat[:], D["emat"], "emat")
        cload("sp", ropecs[:], D["ropecs"].rearrange("(t p) j -> p t j", p=128), "ropecs")
        cload("sp", ropecmpg[:], D["ropecmp"].rearrange("(t p) j -> p t j", p=32), "ropecmpg")
        cload("sp", g1b[:], bcast_rows(D["attn_norm"], 128), "g1b")
        cload("sp", g2b[:], bcast_rows(D["ffn_norm"], 128), "g2b")
        for h in range(8):
            cload("sp", gq_b[:, h, :], bcast_rows(D["nsa_q_norm"], 128), "gq_b")
        for j, r in enumerate((1, 1, 2, 2)):
            cload("sp", gk_b[:, j, :], bcast_rows(D["nsa_k_norm"][r:r + 1, :], 128), "gk_b")
        for j in range(2):
            cload("sp", gkc_b[:, j, :], bcast_rows(D["nsa_k_norm"][0:1, :], 128), "gkc_b")
        for h in range(4):
            cload("sp", go_b[:, h, :], bcast_rows(D["gla_out_norm"], 128), "go_b")
        cload("pool", wg2[:], D["gla_w_gate2"], "wg2")
        cload("pool", bgate[:], D["gla_b_gate"], "bgate")
        cload("pool", w2k[:], D["cmp_w2_k"].rearrange("(a p) d -> p a d", p=128), "w2k")
        cload("pool", w2v[:], D["cmp_w2_v"].rearrange("(a p) d -> p a d", p=128), "w2v")
        cload("sp", ident_f[:], D["ident"], "ident_f")
        cload("pool", pest[:, 0, :], D["cmp_pos_k"], "pest0")
        cload("pool", pest[:, 1, :], D["cmp_pos_v"], "pest1")

        def f(e):
            e.transpose(TR[0:64, 0, 0:32], pest[:, 0, :], ident[0:32, 0:32])
            return e.transpose(TR[0:64, 1, 0:32], pest[:, 1, :], ident[0:32, 0:32])
        p.op("pe", f, reads=["pest0", "pest1", "ident"], writes=["TR"])
        p.op("act", lambda e: e.copy(peT[:], TR[0:64, 0:2, 0:32]), reads=["TR"], writes=["peTk", "peTv"])
        for k_ in range(3):
            cload("sp", cst[:, k_, :], D["conv_w"][k_:k_ + 1, :].rearrange("o (c p) -> (o c) p", p=128), ("cst", k_))
        cload("sp", cst[:, 3, :], D["conv_b"].rearrange("o (c p) -> (o c) p", p=128), ("cst", 3))

        def f(e):
            ins = None
            for k_ in range(4):
                ins = e.transpose(MK[:, k_ * 64:k_ * 64 + 44], cst[:, k_, :], ident_f[0:44, 0:44])
            return ins
        p.op("pe", f, reads=[("cst", k_) for k_ in range(4)] + ["ident_f"], writes=["MK"])
        p.op("act", lambda e: e.copy(convw[:], MK[:, 0:192].rearrange("p (k c) -> p k c", c=64)[:, :, 0:44]), reads=["MK"], writes=["convw"])
        p.op("act", lambda e: e.copy(convb[:], MK[:, 192:236]), reads=["MK"], writes=["convb"])
        p.op("dve", lambda e: e.memset(ones_row[:], 1.0), writes=["ones_row"])
        p.op("dve", lambda e: e.memset(vsx[:], 1.0), writes=["vsx_init"])
        p.op("dve", lambda e: e.memset(vwx[:], 1.0), writes=["vwx_init"])
        p.op("dve", lambda e: e.memset(VcX[:], 1.0), writes=["VcX_init"])
        for g in range(2):
            p.op("pool", (lambda g: lambda e: e.dma_start(out=VcX[:, g, 65:97], in_=D["overlap"]))(g),
                 reads=["VcX_init"], writes=[("VcX_ov", g)], dma=True)
        for r in range(8):
            p.op("pool", (lambda r: lambda e: e.dma_start(out=wup_bf[r * 128:(r + 1) * 128, :], in_=D["w_up"][r * 128:(r + 1) * 128, :]))(r),
                 writes=[("wup_bf", r)], dma=True)
        for r in range(NFC):
            p.op("pool", (lambda r: lambda e: e.dma_start(out=wdn_bf[r * 128:(r + 1) * 128, :], in_=D["w_down"][r * 128:(r + 1) * 128, :]))(r),
                 writes=[("wdn_bf", r)], dma=True)
        WUP_KEYS = [("wup_bf", r) for r in range(8)]

        w1_state = {"n": 0}

        def load_w1_chunk(kv, ch):
            i = w1_state["n"] % 2
            w1_state["n"] += 1
            src = (w1k_bf, w1v_bf)[kv]
            nm = ("w1k", "w1v")[kv]
            view = src[ch * 512:(ch + 1) * 512, :].rearrange("(l d) h -> d l h", d=64)

            def f(e, i=i, view=view):
                a = e.dma_start(out=w1c[i][0:64, :, :], in_=view)
                b = e.dma_start(out=w1c[i][64:128, :, :], in_=view)
                return [a, b]
            p.op("sp", f, reads=[(nm, ch)], writes=[("w1c", i)], dma=True, ndma=2)
            return i

        def c1_chunk(kv, ch, i):
            def f(e):
                ins = None
                for ll in range(8):
                    l = ch * 8 + ll
                    for half in range(2):
                        ins = e.matmul(HB[:, half, 0:1], w1c[i][0:64, ll, half * 128:(half + 1) * 128],
                                       peT[:, kv, l:l + 1], start=(l == 0), stop=(l == 31))
                return ins
            p.op("pe", f, reads=[("w1c", i), "peTk", "peTv"], writes=[("HB", 0), ("HB", 1)])

        for kv in range(2):
            for ch in range(4):
                i = load_w1_chunk(kv, ch)
                c1_chunk(kv, ch, i)
            p.op("act", (lambda kv: lambda e: e.copy(c1[:, kv, :], HB[:, 0:2, 0]))(kv),
                 reads=[("HB", 0), ("HB", 1)], writes=[("c1", kv)])

        p.wait_all("pool", [o for o in p.ops["pool"] if o.dma])

        def rstd_from_ss(ap, scale, key):
            p.op("act", lambda e: e.activation(ap, ap, AF.Sqrt, bias=EPS, scale=scale), reads=[key], writes=[key])
            p.op("dve", lambda e: e.reciprocal(ap, ap), reads=[key], writes=[key])

        x_state = {"n": 0}

        def load_x(s, tt):
            i = x_state["n"] % 2
            x_state["n"] += 1
            r0 = s * S + tt * 128
            p.op("sp", lambda e: e.dma_start(out=xt[i][:], in_=D["x"][r0:r0 + 128, :]), writes=[("xt", i)], dma=True)
            return i

        wing_state = {"n": 0}
        WIN_KEYS = [("win_bf", d_) for (d_, _, _) in WIN_PIECES]

        def load_wing(gi):
            i = wing_state["n"] % 2
            wing_state["n"] += 1
            c0, w = WIN_GROUPS[gi]
            p.op("sp", lambda e: e.dma_start(out=wing[i][:, :, 0:w], in_=win_bf[:, c0:c0 + w].rearrange("(k p) c -> p k c", p=128)),
                 reads=WIN_KEYS, writes=[("wing", i)], dma=True)
            return i

        def norm_rope(src_ap, nh, gain_ap, gain_key, pre_scale, cs_ap, cs_key, src_keys, P=128):
            sq_v = sq[0:P, 0:nh * 64].rearrange("p (h d) -> p h d", d=64)
            qn_v = qn[0:P, 0:nh * 64].rearrange("p (h d) -> p h d", d=64)
            ss = st4[0:P, 0:nh]
            p.op("act", lambda e: e.activation(sq_v, src_ap, AF.Square), reads=src_keys, writes=["sq"])
            p.op("dve", lambda e: e.tensor_reduce(ss, sq_v, AX.X, ALU.add), reads=["sq"], writes=["st4"])
            rstd_from_ss(ss, 1.0 / 64.0, "st4")
            p.op("dve", lambda e: e.scalar_tensor_tensor(qn_v, src_ap, float(pre_scale), gain_ap, ALU.mult, ALU.mult),
                 reads=list(src_keys) + [gain_key], writes=["qn"])
            p.op("dve", lambda e: e.tensor_tensor(qn_v, qn_v, ss.unsqueeze(2).broadcast_to([P, nh, 64]), ALU.mult),
                 reads=["qn", "st4"], writes=["qn"])
            cosb = cs_ap[:, 0:8].unsqueeze(1).unsqueeze(1).broadcast_to([P, nh, 2, 8])
            sinb = cs_ap[:, 8:16].unsqueeze(1).broadcast_to([P, nh, 8])
            rA = ropeA[0:P, 0:nh, :]
            rB = ropeB[0:P, 0:nh, :]
            p.op("pool", lambda e: e.tensor_tensor(rA.rearrange("p h (t d) -> p h t d", t=2), qn_v[:, :, 0:16].rearrange("p h (t d) -> p h t d", t=2), cosb, ALU.mult),
                 reads=["qn", cs_key], writes=["ropeA"])
            p.op("pool", lambda e: e.tensor_tensor(rB[:, :, 0:8], qn_v[:, :, 8:16], sinb, ALU.mult), reads=["qn", cs_key], writes=["ropeB0"])
            p.op("pool", lambda e: e.tensor_tensor(rB[:, :, 8:16], qn_v[:, :, 0:8], sinb, ALU.mult), reads=["qn", cs_key], writes=["ropeB1"])
            p.op("pool", lambda e: e.tensor_tensor(qn_v[:, :, 0:8], rA[:, :, 0:8], rB[:, :, 0:8], ALU.subtract),
                 reads=["ropeA", "ropeB0", "ropeB1"], writes=["qn"])
            p.op("pool", lambda e: e.tensor_tensor(qn_v[:, :, 8:16], rA[:, :, 8:16], rB[:, :, 8:16], ALU.add),
                 reads=["ropeA", "ropeB0", "ropeB1"], writes=["qn"])
            return qn_v

        tr_state = {"n": 0}

        def tr_half():
            i = tr_state["n"] % 2
            tr_state["n"] += 1
            return i

        def transposes(srcs, src_keys, P_out=128, idn=None):
            th = tr_half()
            idn_ = ident[:] if idn is None else idn

            def f(e):
                ins = None
                for j, a in enumerate(srcs):
                    m = a.shape[1]
                    kk = a.shape[0]
                    ins = e.transpose(TR[0:m, th * 4 + j, 0:kk], a, idn_[0:kk, 0:kk])
                return ins
            p.op("pe", f, reads=list(src_keys) + ["ident"], writes=["TR"])
            return th

        HK = [("HB", b) for b in range(4)]

        def g1_xnorm(s, tg, i4):
            import os
            parts = os.environ.get("G1PARTS", "xsrntc")
            tt = tg * 4 + i4
            xi = load_x(s, tt) if "x" in parts else 0
            if "s" in parts:
                p.op("act", lambda e: e.activation(junk[:], xt[xi][:], AF.Square, accum_out=st4[:, 8:9]),
                     reads=[("xt", xi)], writes=["junk", "st4x"])
            if "r" in parts:
                rstd_from_ss(st4[:, 8:9], 1.0 / DM, "st4x")
            if "n" in parts:
                p.op("dve", lambda e: e.scalar_tensor_tensor(xn[:], xt[xi][:], st4[:, 8:9], g1b[:], ALU.mult, ALU.mult),
                     reads=[("xt", xi), "st4x", "g1b"], writes=["xn"])
            for hh in range(2):
                if "t" in parts:
                    th = transposes([xn[:, (hh * 4 + k) * 128:(hh * 4 + k + 1) * 128] for k in range(4)], ["xn"])
                else:
                    th = 0
                if "c" in parts:
                    p.op("act", (lambda hh, th: lambda e: e.copy(xnT[:, i4, hh * 4:(hh + 1) * 4, :], TR[:, th * 4:(th + 1) * 4, :]))(hh, th),
                         reads=["TR"], writes=[("xnT", i4, hh)])

        def g1_z(s, tg, gi, i4, wi):
            tt = tg * 4 + i4
            c0, w = WIN_GROUPS[gi]
            zb = i4
            Z = HB[:, zb, :]
            ZK = [("HB", zb)]

            def f(e):
                ins = None
                for k in range(8):
                    ins = e.matmul(HB[:, zb, 0:w], xnT[:, i4, k, :], wing[wi][:, k, 0:w], start=(k == 0), stop=(k == 7))
                return ins
            p.op("pe", f, reads=[("wing", wi), ("xnT", i4, 0), ("xnT", i4, 1)], writes=ZK)
            if gi == 0:
                qn_v = norm_rope(Z.rearrange("p (h d) -> p h d", d=64), 8, gq_b[:], "gq_b", 0.125, ropecs[:, tt, :], "ropecs", ZK)
                p.op("act", lambda e: e.copy(qb[:].rearrange("p h g d -> p g h d"), qn[:, 0:512].rearrange("p (g h d) -> p g h d", g=2, h=4)),
                     reads=["qn"], writes=["qb"])
                th = transposes([qb[:, h, :, :].rearrange("p g d -> p (g d)") for h in range(4)], ["qb"])
                p.op("act", lambda e: e.copy(qTg[:, i4, :, :], TR[:, th * 4:(th + 1) * 4, :]), reads=["TR"], writes=[("qTg", i4)])
            elif gi == 1:
                p.op("act", lambda e: e.copy(kvb[:], Z[:, 0:256]), reads=ZK, writes=["kvb"])
                th = transposes([kvb[:, 0:128], kvb[:, 128:256]], ["kvb"])
                c_ = 16 + i4 * 128
                p.op("act", lambda e: e.copy(kcTg[:, c_:c_ + 128], TR[:, th * 4 + 0, :]), reads=["TR"], writes=["kcTg"])
                p.op("dve", lambda e: e.tensor_copy(vcTg[:, c_:c_ + 128], TR[:, th * 4 + 1, :]), reads=["TR"], writes=["vcTg"])
                p.op("act", lambda e: e.copy(vsx[:, tt, :, 0:64], Z[:, 256:384].rearrange("p (g d) -> p g d", d=64)),
                     reads=ZK + ["vsx_init"], writes=[("vsx", tt)])
                p.op("dve", lambda e: e.tensor_copy(vwx[:, tt, :, 0:64], Z[:, 384:512].rearrange("p (g d) -> p g d", d=64)),
                     reads=ZK + ["vwx_init"], writes=[("vwx", tt)])
            elif gi == 2:
                qn_v = norm_rope(Z[:, 0:256].rearrange("p (h d) -> p h d", d=64), 4, gk_b[:], "gk_b", 1.0, ropecs[:, tt, :], "ropecs", ZK)
                p.op("act", lambda e: e.copy(kb[:], qn_v), reads=["qn"], writes=["kb"])
                th = transposes([kb[:, 0:2, :].rearrange("p g d -> p (g d)"), kb[:, 2:4, :].rearrange("p g d -> p (g d)")], ["kb"])
                p.op("act", lambda e: e.copy(ksT[:, tt * 128:(tt + 1) * 128], TR[:, th * 4 + 0, :]), reads=["TR"], writes=[("ksT", tt)])
                p.op("dve", lambda e: e.tensor_copy(kwT[:, tt * 128:(tt + 1) * 128], TR[:, th * 4 + 1, :]), reads=["TR"], writes=[("kwT", tt)])
                p.op("act", lambda e: e.activation(sigg[:, i4, :], Z[:, 256:280], AF.Sigmoid), reads=ZK, writes=[("sigg", i4)])
                p.op("act", lambda e: e.copy(glrb[:], Z[:, 280:296]), reads=ZK, writes=["glrb"])
                th2 = transposes([glrb[:]], ["glrb"])
                p.op("act", lambda e: e.copy(glrT[:], TR[0:16, th2 * 4, :]), reads=["TR"], writes=["glrT"])

                def f(e):
                    e.matmul(MK[:, 0:256], glrT[:], wg2[:], start=True, stop=False)
                    return e.matmul(MK[:, 0:256], ones_row[:], bgate[:], start=False, stop=True)
                p.op("pe", f, reads=["glrT", "wg2", "bgate", "ones_row"], writes=["MK"])
                p.op("act", lambda e: e.activation(e1[:], MK[:, 0:256], AF.Exp, scale=-1.0), reads=["MK"], writes=["e1"])
                for c in range(2):
                    p.op("act", (lambda c: lambda e: e.activation(lsp2[:, :, c, :], e1[:].rearrange("p (h d) -> p h d", d=64), AF.Ln, bias=1.0))(c),
                         reads=["e1"], writes=[("lsp2", c)])
                LS = [("lsp2", 0), ("lsp2", 1)]

                def f(e):
                    e.matmul(MK[:, 0:256].rearrange("p (h d) -> p h d", d=64), tcum[:], lsp2[:, :, 0, :], start=True, stop=True)
                    return e.matmul(MK[:, 256:512].rearrange("p (h d) -> p h d", d=64), tafter[:], lsp2[:, :, 0, :], start=True, stop=True)
                p.op("pe", f, reads=LS + ["tcum", "tafter"], writes=["MK"])
                p.op("act", lambda e: e.activation(eB[:, i4, 0, :], MK[:, 0:256], AF.Exp, scale=-1.0 / 16.0, bias=LN8), reads=["MK"], writes=[("eB", i4, 0)])
                p.op("act", lambda e: e.activation(eB[:, i4, 1, :], MK[:, 0:256], AF.Exp, scale=1.0 / 16.0), reads=["MK"], writes=[("eB", i4, 1)])
                p.op("act", lambda e: e.activation(eB[:, i4, 2, :], MK[:, 256:512], AF.Exp, scale=-1.0 / 16.0), reads=["MK"], writes=[("eB", i4, 2)])

                def f(e):
                    ins = None
                    for h in range(4):
                        ins = e.matmul(MK[:, 2 * h:2 * h + 2], lsp2[:, h, :, :].rearrange("p c d -> p (c d)"), chunkind[:], start=True, stop=True)
                    return ins
                p.op("pe", f, reads=LS + ["chunkind"], writes=["MK"])
                p.op("act", lambda e: e.activation(decayg[:, i4, :, :].rearrange("p h c -> p (h c)"), MK[:, 0:8], AF.Exp, scale=-1.0 / 16.0),
                     reads=["MK"], writes=[("decayg", i4)])
            elif gi == 3:
                def v3(ap):
                    return ap.rearrange("p (h d) -> p h d", d=64)
                for c in range(2):
                    p.op("dve", (lambda c: lambda e: e.scalar_tensor_tensor(qtb2[:, :, c, :], v3(Z[:, 0:256]), chunkind[:, c:c + 1], v3(eB[:, i4, 0, :]), ALU.mult, ALU.mult))(c),
                         reads=ZK + [("eB", i4, 0), "chunkind"], writes=[("qtb2", c)])
                    p.op("dve", (lambda c: lambda e: e.tensor_tensor(ktb2[:, :, c, :], v3(Z[:, 256:512]), v3(eB[:, i4, 1, :]), ALU.mult))(c),
                         reads=ZK + [("eB", i4, 1)], writes=[("ktb2", c)])
                    p.op("dve", (lambda c: lambda e: e.tensor_tensor(kdec2[:, i4, :, c, :], v3(Z[:, 256:512]), v3(eB[:, i4, 2, :]), ALU.mult))(c),
                         reads=ZK + [("eB", i4, 2)], writes=[("kdec2", i4, c)])
                for qk, srcb, key in ((0, qtb2, "qtb2"), (1, ktb2, "ktb2")):
                    th = transposes([srcb[:, h, :, :].rearrange("p c d -> p (c d)") for h in range(4)], [(key, 0), (key, 1)])
                    p.op("act", (lambda th, qk: lambda e: e.copy(qkT[:, i4, qk * 4:(qk + 1) * 4, :], TR[:, th * 4:(th + 1) * 4, :]))(th, qk),
                         reads=["TR"], writes=[("qkT", i4, qk)])
            elif gi == 4:
                p.op("act", lambda e: e.copy(gvb[:, i4, :], Z), reads=ZK, writes=[("gvb", i4)])
            elif gi == 5:
                p.op("act", lambda e: e.activation(sgo[:, i4, :], Z, AF.Silu), reads=ZK, writes=[("sgo", i4)])
                p.op("pool", lambda e: e.tensor_tensor(sgo[:, i4, :].rearrange("p (h v) -> p h v", v=128), sgo[:, i4, :].rearrange("p (h v) -> p h v", v=128), go_b[:], ALU.mult),
                     reads=[("sgo", i4), "go_b"], writes=[("sgo", i4)])

        def do_g1(s, tg):
            import os
            wi = load_wing(0) if "w" in os.environ.get("G1PARTS", "w") else 0
            for i4 in range(4):
                g1_xnorm(s, tg, i4)
            for gi in range(min(6, G1_MAXGI[0])):
                wi_next = load_wing(gi + 1) if gi < 5 else None
                for i4 in range(4):
                    g1_z(s, tg, gi, i4, wi)
                wi = wi_next

        def g2_first(kv, ch, wi1):
            srcT = (kcTg, vcTg)[kv]
            skey = ("kcTg", "vcTg")[kv]

            def f(e):
                ins = None
                for ll in range(8):
                    l = ch * 8 + ll
                    for g in range(2):
                        rhs = srcT[64 * g:64 * g + 64, l:l + 512].rearrange("p (n s) -> p n s", s=16)[:, :, 0]
                        for half in range(2):
                            ins = e.matmul(HB[:, g * 2 + half, 0:32], w1c[wi1][64 * g:64 * g + 64, ll, half * 128:(half + 1) * 128],
                                           rhs, start=(l == 0), stop=(l == 31))
                return ins
            p.op("pe", f, reads=[("w1c", wi1), skey], writes=HK)

        def g2_gelu(kv, g, half):
            p.op("act", lambda e: e.activation(hTc[:, kv, g, half, :], HB[:, g * 2 + half, 0:32], AF.Gelu_apprx_tanh, bias=c1[:, kv, half:half + 1]),
                 reads=[("HB", g * 2 + half), ("c1", kv)], writes=[("hTc", kv, g, half)])

        def do_g2(s, tg):
            for kv in range(2):
                for ch in range(4):
                    wi1 = load_w1_chunk(kv, ch)
                    g2_first(kv, ch, wi1)
                for g in range(2):
                    for half in range(2):
                        g2_gelu(kv, g, half)
            p.op("pool", lambda e: e.tensor_copy(kcTg[:, 0:16], kcTg[:, 512:528]), reads=["kcTg"], writes=["kcTg"])
            p.op("pool", lambda e: e.tensor_copy(vcTg[:, 0:16], vcTg[:, 512:528]), reads=["vcTg"], writes=["vcTg"])

            def f(e):
                ins = None
                for g in range(2):
                    for half in range(2):
                        ins = e.matmul(MK[0:32, g * 64:(g + 1) * 64], hTc[:, 0, g, half, :], w2k[:, half, :], start=(half == 0), stop=(half == 1))
                for g in range(2):
                    for half in range(2):
                        ins = e.matmul(MK[0:64, 128 + g * 32:128 + (g + 1) * 32], w2v[:, half, :], hTc[:, 1, g, half, :], start=(half == 0), stop=(half == 1))
                return ins
            p.op("pe", f, reads=[("hTc", kv, g, half) for kv in range(2) for g in range(2) for half in range(2)] + ["w2k", "w2v"], writes=["MK"])
            p.op("act", lambda e: e.copy(vcmpT[:, :, tg * 32:(tg + 1) * 32], MK[0:64, 128:192].rearrange("p (g n) -> p g n", n=32)),
                 reads=["MK"], writes=["vcmpT"])
            qn_v = norm_rope(MK[0:32, 0:128].rearrange("p (g d) -> p g d", d=64), 2, gkc_b[0:32], "gkc_b", 1.0, ropecmpg[:, tg, :], "ropecmpg", ["MK"], P=32)
            p.op("act", lambda e: e.copy(kcb[:], qn_v), reads=["qn"], writes=["kcb"])
            th = transposes([kcb[:].rearrange("p g d -> p (g d)")], ["kcb"])
            p.op("act", lambda e: e.copy(kcmpT[:, tg * 32:(tg + 1) * 32], TR[:, th * 4, 0:32]), reads=["TR"], writes=["kcmpT"])
            th2 = transposes([vcmpT[:, 0, :], vcmpT[:, 1, :]], ["vcmpT"])
            p.op("act", lambda e: e.copy(VcX[:, :, 0:64], TR[:, th2 * 4:th2 * 4 + 2, 0:64]),
                 reads=["TR", "VcX_init", ("VcX_ov", 0), ("VcX_ov", 1)], writes=["VcX"])
            dbg("qTg_%d_%d" % (s, tg), qTg[:], [128, 4, 4, 128], [("qTg", i) for i in range(4)])
            dbg("kcmpT_%d_%d" % (s, tg), kcmpT[:], [128, 128], ["kcmpT"])
            dbg("VcX_%d_%d" % (s, tg), VcX[:], [128, 2, 97], ["VcX"])
            dbg("ksT_%d_%d" % (s, tg), ksT[:], [128, S], [("ksT", t_) for t_ in range(tg * 4 + 4)])
            dbg("vsx_%d_%d" % (s, tg), vsx[:], [128, NT, 2, 65], [("vsx", t_) for t_ in range(tg * 4 + 4)])
            dbg("sgo_%d_%d" % (s, tg), sgo, [128, 4, 512], [("sgo", i) for i in range(4)])
            dbg("qkT_%d_%d" % (s, tg), qkT[:], [128, 4, 8, 128], [("qkT", i, j) for i in range(4) for j in range(2)])
            dbg("decayg_%d_%d" % (s, tg), decayg[:], [128, 4, 4, 2], [("decayg", i) for i in range(4)])

        def gla_core(s, tg, i4):
            tt = tg * 4 + i4

            def f(e):
                ins = None
                for h in range(4):
                    ins = e.matmul(SC[:, 0, h * 128:(h + 1) * 128], qkT[:, i4, 4 + h, :], qkT[:, i4, h, :], start=True, stop=True)
                return ins
            p.op("pe", f, reads=[("qkT", i4, 0), ("qkT", i4, 1)], writes=[("SC", 0)])
            p.op("dve", lambda e: e.tensor_tensor(ATm[:], SC[:, 0, :].rearrange("p (h i) -> p h i", i=128),
                                                  tcum_bf[:].unsqueeze(1).broadcast_to([128, 4, 128]), ALU.mult),
                 reads=[("SC", 0), "tcum_bf"], writes=["ATm"])

            def dS(c):
                def f(e):
                    ins = None
                    for h in range(4):
                        ins = e.matmul(MK[:, h * 128:(h + 1) * 128], kdec2[64 * c:64 * c + 64, i4, h, :, :].rearrange("p c d -> p (c d)"),
                                       gvb[64 * c:64 * c + 64, i4, 128 * h:128 * h + 128], start=True, stop=True)
                    return ins
                p.op("pe", f, reads=[("kdec2", i4, 0), ("kdec2", i4, 1), ("gvb", i4)], writes=["MK"])

            def state_update(c):
                dec = decayg[:, i4, :, c:c + 1].broadcast_to([128, 4, 128])
                p.op("dve", lambda e: e.tensor_tensor(Sst[:], Sst[:], dec, ALU.mult), reads=["Sst", ("decayg", i4)], writes=["Sst"])
                p.op("dve", lambda e: e.tensor_tensor(Sst[:], Sst[:], MK[:, :].rearrange("p (h v) -> p h v", v=128), ALU.add),
                     reads=["Sst", "MK"], writes=["Sst"])

            dS(0)
            state_update(0)
            p.op("act", lambda e: e.copy(S2_bf[64:128, :, :], Sst[64:128, :, :]), reads=["Sst"], writes=["S2_hi"])

            def f(e):
                ins = None
                for h in range(4):
                    e.matmul(SC[:, 1, h * 128:(h + 1) * 128], ATm[:, h, :], gvb[:, i4, 128 * h:128 * h + 128], start=True, stop=False)
                    ins = e.matmul(SC[:, 1, h * 128:(h + 1) * 128], qkT[:, i4, h, :], S2_bf[:, h, :], start=False, stop=True)
                return ins
            p.op("pe", f, reads=["ATm", ("gvb", i4), ("qkT", i4, 0), "S2_lo", "S2_hi"], writes=[("SC", 1)])
            dS(1)
            state_update(1)
            p.op("act", lambda e: e.copy(S2_bf[0:64, :, :], Sst[0:64, :, :]), reads=["Sst"], writes=["S2_lo"])
            O3 = SC[:, 1, :].rearrange("p (h v) -> p h v", v=128)
            p.op("act", lambda e: e.activation(sq[:], SC[:, 1, :], AF.Square), reads=[("SC", 1)], writes=["sq"])
            p.op("dve", lambda e: e.tensor_reduce(st4[:, 0:4], sq[:].rearrange("p (h v) -> p h v", v=128), AX.X, ALU.add), reads=["sq"], writes=["st4"])
            rstd_from_ss(st4[:, 0:4], 1.0 / 128.0, "st4")
            p.op("dve", lambda e: e.tensor_tensor(og[:], O3, st4[:, 0:4].unsqueeze(2).broadcast_to([128, 4, 128]), ALU.mult),
                 reads=[("SC", 1), "st4"], writes=["og"])
            p.op("pool", lambda e: e.tensor_tensor(obf[:, 512:1024], og[:].rearrange("p h v -> p (h v)"), sgo[:, i4, :], ALU.mult),
                 reads=["og", ("sgo", i4)], writes=["obf_gla"])
            if i4 == 1:
                dbg("og_%d_%d" % (s, tt), og[:], [128, 4, 128], ["og"])

        pb_state = {"n": 0}
        sc_state = {"n": 0}

        def nsa_group(s, tg, i4, g):
            tt = tg * 4 + i4
            gp0 = 64 * g
            qrhs = qTg[gp0:gp0 + 64, i4, :, :]

            def score_exp(kT_ap, kkeys):
                pi = pb_state["n"] % 3
                pb_state["n"] += 1
                sc_i = sc_state["n"] % 2
                sc_state["n"] += 1
                p.op("pe", lambda e: e.matmul(SC[:, sc_i, :].rearrange("p (h q) -> p h q", q=128), kT_ap, qrhs, start=True, stop=True),
                     reads=kkeys + [("qTg", i4)], writes=[("SC", sc_i)])
                p.op("act", lambda e: e.activation(Pb[pi][:], SC[:, sc_i, :].rearrange("p (h q) -> p h q", q=128), AF.Exp),
                     reads=[("SC", sc_i)], writes=[("Pb", pi)])
                return pi

            def pmask(pi, m_ap, mkeys):
                p.op("dve", lambda e: e.tensor_tensor(Pb[pi][:], Pb[pi][:], m_ap.unsqueeze(1).broadcast_to([128, 4, 128]), ALU.mult),
                     reads=[("Pb", pi)] + mkeys, writes=[("Pb", pi)])

            def pv(pi, v_ap, vkeys, col0, ncol, first, last):
                def f(e):
                    ins = None
                    for h in range(4):
                        ins = e.matmul(HB[:, h, col0:col0 + ncol], Pb[pi][:, h, :], v_ap, start=first, stop=last)
                    return ins
                p.op("pe", f, reads=[("Pb", pi)] + vkeys, writes=HK)

            pi = score_exp(kcmpT[gp0:gp0 + 64, :], ["kcmpT"])
            pmask(pi, cmpmask[:, tt * 128:(tt + 1) * 128], ["cmpmask"])
            pv(pi, VcX[:, g, :], ["VcX"], 130, 97, True, True)
            p.op("dve", lambda e: e.tensor_scalar(rinv[:, 0, :], HB[:, :, 194], 1e-30, None, ALU.max), reads=HK, writes=[("rinv", 0)])
            p.op("dve", lambda e: e.reciprocal(rinv[:, 0, :], rinv[:, 0, :]), reads=[("rinv", 0)], writes=[("rinv", 0)])
            for h in range(4):
                p.op("dve", (lambda h: lambda e: e.scalar_tensor_tensor(score[:], HB[:, h, 195:227], rinv[:, 0, h:h + 1],
                                                                         selbias[:, tt, :] if h == 0 else score[:], ALU.mult, ALU.add))(h),
                     reads=HK + [("rinv", 0), "selbias", "score"], writes=["score"])
            p.op("dve", lambda e: e.max(top8[:], score[:]), reads=["score"], writes=["top8"])
            p.op("dve", lambda e: e.tensor_scalar(selm[:], score[:], top8[:, 7:8], None, ALU.is_ge), reads=["score", "top8"], writes=["selm"])
            th = transposes([selm[:]], ["selm"])
            p.op("act", lambda e: e.copy(selT[:], TR[0:32, th * 4, :]), reads=["TR"], writes=["selT"])
            if g == 0 and i4 == 1:
                dbg("score_%d_%d" % (s, tt), score[:], [128, 32], ["score"])
                dbg("selm_%d_%d" % (s, tt), selm[:], [128, 32], ["selm"])
                dbg("hbcmp_%d_%d" % (s, tt), HB[:, :, 130:227], [128, 4, 97], HK)
            for kt in range(tt + 1):
                if kt % 4 == 0:
                    nk = min(4, tt + 1 - kt)

                    def f(e, kt=kt, nk=nk):
                        ins = None
                        for j in range(nk):
                            ins = e.matmul(MK[:, j * 128:(j + 1) * 128], emat[:, (kt + j) * 128:(kt + j + 1) * 128], selT[:], start=True, stop=True)
                        return ins
                    p.op("pe", f, reads=["emat", "selT"], writes=["MK"])
                pi = score_exp(ksT[gp0:gp0 + 64, kt * 128:(kt + 1) * 128], [("ksT", kt)])
                pmask(pi, MK[:, (kt % 4) * 128:(kt % 4 + 1) * 128], ["MK"])
                if kt == tt:
                    pmask(pi, cm[:], ["cm"])
                pv(pi, vsx[:, kt, g, :], [("vsx", kt)], 0, 65, kt == 0, kt == tt)
            kts = [k_ for k_ in range(tt - 4, tt + 1) if k_ >= 0]
            for kt in kts:
                pi = score_exp(kwT[gp0:gp0 + 64, kt * 128:(kt + 1) * 128], [("kwT", kt)])
                if kt == tt - 4:
                    pmask(pi, bm[:], ["bm"])
                if kt == tt:
                    pmask(pi, cm[:], ["cm"])
                pv(pi, vwx[:, kt, g, :], [("vwx", kt)], 65, 65, kt == kts[0], kt == tt)
            if g == 0 and i4 == 1:
                dbg("hball_%d_%d" % (s, tt), HB[:, :, 0:227], [128, 4, 227], HK)
            p.op("dve", lambda e: e.reciprocal(rinv[:, 1, :], HB[:, :, 64]), reads=HK, writes=[("rinv", 1)])
            p.op("dve", lambda e: e.reciprocal(rinv[:, 2, :], HB[:, :, 129]), reads=HK, writes=[("rinv", 2)])
            sgv = sigg[:, i4, g * 12:(g + 1) * 12].rearrange("p (h b) -> p b h", b=3)
            p.op("dve", lambda e: e.tensor_tensor(gf[:], rinv[:], sgv, ALU.mult),
                 reads=[("rinv", 0), ("rinv", 1), ("rinv", 2), ("sigg", i4)], writes=["gf"])
            for b, col0 in ((0, 130), (1, 0), (2, 65)):
                dst = oacc if b == 0 else otmp
                p.op("dve", (lambda b, col0, dst: lambda e: e.tensor_tensor(dst[:], HB[:, :, col0:col0 + 64], gf[:, b, :].unsqueeze(2).broadcast_to([128, 4, 64]), ALU.mult))(b, col0, dst),
                     reads=HK + ["gf"], writes=["oacc" if b == 0 else "otmp"])
                if b == 1:
                    p.op("pool", lambda e: e.tensor_tensor(oacc[:], oacc[:], otmp[:], ALU.add), reads=["oacc", "otmp"], writes=["oacc"])
                if b == 2:
                    p.op("pool", lambda e: e.tensor_tensor(obf[:, g * 256:(g + 1) * 256].rearrange("p (h d) -> p h d", d=64), oacc[:], otmp[:], ALU.add),
                         reads=["oacc", "otmp"], writes=[("obf_nsa", g)])

        wo_state = {"n": 0}

        def out_proj(s, tg, i4):
            tt = tg * 4 + i4
            OB = [("obf_nsa", 0), ("obf_nsa", 1), "obf_gla"]
            for hh in range(2):
                th = transposes([obf[:, (hh * 4 + k) * 128:(hh * 4 + k + 1) * 128] for k in range(4)], OB)
                p.op("act", (lambda hh, th: lambda e: e.copy(oT[:, hh * 4:(hh + 1) * 4, :], TR[:, th * 4:(th + 1) * 4, :]))(hh, th),
                     reads=["TR"], writes=[("oT", hh)])
            if i4 == 1:
                dbg("obf_%d_%d" % (s, tt), obf[:], [128, DM], OB)
            xi = load_x(s, tt)
            for k in range(8):
                slot = wo_state["n"] % 2
                wo_state["n"] += 1
                p.op("sp", (lambda k, slot: lambda e: e.dma_start(out=woutc[slot][:], in_=wout_bf[k * 128:(k + 1) * 128, :]))(k, slot),
                     reads=[("wout_bf", k)], writes=[("woutc", slot)], dma=True)

                def f(e, k=k, slot=slot):
                    e.matmul(SC[:, 0, :], oT[:, k, :], woutc[slot][:, 0:512], start=(k == 0), stop=(k == 7))
                    return e.matmul(SC[:, 1, :], oT[:, k, :], woutc[slot][:, 512:1024], start=(k == 0), stop=(k == 7))
                p.op("pe", f, reads=[("oT", 0), ("oT", 1), ("woutc", slot)], writes=[("SC", 0), ("SC", 1)])
            for half in range(2):
                p.op("dve", (lambda half: lambda e: e.tensor_tensor(hres[:, i4, half * 512:(half + 1) * 512], SC[:, half, :], xt[xi][:, half * 512:(half + 1) * 512], ALU.add))(half),
                     reads=[("SC", half), ("xt", xi)], writes=[("hres", i4, half)])
            HR = [("hres", i4, 0), ("hres", i4, 1)]
            p.op("act", lambda e: e.activation(junk[:], hres[:, i4, :], AF.Square, accum_out=st4[:, 9:10]), reads=HR, writes=["junk", "st4h"])
            rstd_from_ss(st4[:, 9:10], 1.0 / DM, "st4h")
            p.op("dve", lambda e: e.scalar_tensor_tensor(hn[:], hres[:, i4, :], st4[:, 9:10], g2b[:], ALU.mult, ALU.mult),
                 reads=HR + ["st4h", "g2b"], writes=["hn"])
            for hh in range(2):
                th = transposes([hn[:, (hh * 4 + k) * 128:(hh * 4 + k + 1) * 128] for k in range(4)], ["hn"])
                p.op("act", (lambda hh, th: lambda e: e.copy(hnT[:, hh * 4:(hh + 1) * 4, i4 * 128:(i4 + 1) * 128], TR[:, th * 4:(th + 1) * 4, :]))(hh, th),
                     reads=["TR"], writes=[("hnT", i4)])

        WUP_KEYS = [("wup_bf", r) for r in range(8)]
        HN = [("hnT", i) for i in range(4)]

        def load_wup(fc, slot):
            def f(e):
                a = e.dma_start(out=wupc[slot][:, :, 0:128], in_=wup_bf[:, fc * 128:(fc + 1) * 128].rearrange("(k p) c -> p k c", p=128))
                b = e.dma_start(out=wupc[slot][:, :, 128:256], in_=wup_bf[:, DFF + fc * 128:DFF + (fc + 1) * 128].rearrange("(k p) c -> p k c", p=128))
                return [a, b]
            p.op("sp", f, reads=WUP_KEYS, writes=[("wupc", slot)], dma=True, ndma=2)

        def load_wdn(fc, slot):
            p.op("sp", lambda e: e.dma_start(out=wdnc[slot][:], in_=wdn_bf[fc * 128:(fc + 1) * 128, :]), reads=[("wdn_bf", fc)], writes=[("wdnc", slot)], dma=True)

        def ffn_up_chunk(fc, slot, gu):
            bank = slot * 2 + gu

            def f(e):
                ins = None
                for k in range(8):
                    ins = e.matmul(HB[:, bank, :], wupc[slot][:, k, gu * 128:(gu + 1) * 128], hnT[:, k, :], start=(k == 0), stop=(k == 7))
                return ins
            p.op("pe", f, reads=[("wupc", slot)] + HN, writes=[("HB", bank)])
            ub = ubuf[slot][gu]
            ukey = ("ub", slot, gu)
            cidx = gu * NFC + fc
            p.op("pool", lambda e: e.tensor_copy(ub[:, 0:2], halo[:, cidx, :]), reads=[("halo", cidx)], writes=[ukey])
            p.op("act", lambda e: e.copy(ub[:, 2:514], HB[:, bank, :]), reads=[("HB", bank)], writes=[ukey])
            p.op("pool", lambda e: e.tensor_copy(halo[:, cidx, :], ub[:, 512:514]), reads=[ukey], writes=[("halo", cidx)])
            yb = ybuf[gu]
            ykey = ("yb", gu)
            p.op("dve", lambda e: e.tensor_scalar(yb[:], ub[:, 2:514], convw[:, 2, cidx:cidx + 1], convb[:, cidx:cidx + 1], ALU.mult, ALU.add),
                 reads=[ukey, "convw", "convb"], writes=[ykey])
            p.op("dve", lambda e: e.scalar_tensor_tensor(yb[:], ub[:, 1:513], convw[:, 1, cidx:cidx + 1], yb[:], ALU.mult, ALU.add),
                 reads=[ukey, ykey, "convw"], writes=[ykey])
            p.op("dve", lambda e: e.scalar_tensor_tensor(yb[:], ub[:, 0:512], convw[:, 0, cidx:cidx + 1], yb[:], ALU.mult, ALU.add),
                 reads=[ukey, ykey, "convw"], writes=[ykey])

        def ffn_act(fc):
            p.op("act", lambda e: e.activation(ybuf[0][:], ybuf[0][:], AF.Silu), reads=[("yb", 0)], writes=[("yb", 0)])
            p.op("dve", lambda e: e.tensor_tensor(actT[:, fc, :], ybuf[0][:], ybuf[1][:], ALU.mult), reads=[("yb", 0), ("yb", 1)], writes=[("actT", fc)])

        def ffn_down(s, tg, pair):
            load_wdn(0, 0)
            for fc in range(NFC):
                slot = fc % 2
                if fc + 1 < NFC:
                    load_wdn(fc + 1, (fc + 1) % 2)

                def f(e, fc=fc, slot=slot):
                    ins = None
                    for t2 in range(2):
                        i4 = pair * 2 + t2
                        for half in range(2):
                            ins = e.matmul(HB[:, t2 * 2 + half, :], actT[:, fc, i4 * 128:(i4 + 1) * 128], wdnc[slot][:, half * 512:(half + 1) * 512],
                                           start=(fc == 0), stop=(fc == NFC - 1))
                    return ins
                p.op("pe", f, reads=[("actT", fc), ("wdnc", slot)], writes=HK)
            for t2 in range(2):
                i4 = pair * 2 + t2
                tt = tg * 4 + i4
                HR = [("hres", i4, 0), ("hres", i4, 1)]
                p.op("dve", (lambda t2, i4: lambda e: e.tensor_tensor(hres[:, i4, :].rearrange("p (a c) -> p a c", a=2), HB[:, t2 * 2:t2 * 2 + 2, :],
                                                                      hres[:, i4, :].rearrange("p (a c) -> p a c", a=2), ALU.add))(t2, i4),
                     reads=HK + HR, writes=HR)
                r0 = s * S + tt * 128
                p.op("sp", (lambda i4, r0: lambda e: e.dma_start(out=out[r0:r0 + 128, :], in_=hres[:, i4, :]))(i4, r0), reads=HR, dma=True)

        REGION_KEYS = ([("wing", i) for i in range(2)] + [("xnT", i, h) for i in range(4) for h in range(2)]
                       + [("actT", fc) for fc in range(NFC)]
                       + [("eB", i, j) for i in range(4) for j in range(3)] + [("sgo", i) for i in range(4)] + [("gvb", i) for i in range(4)]
                       + [("ub", i, j) for i in range(2) for j in range(2)] + [("yb", j) for j in range(2)]
                       + [("wupc", i) for i in range(2)] + [("wdnc", i) for i in range(2)])

        def region_barrier():
            p.op("pool", lambda e: e.memset(dummy[:], 0.0), writes=REGION_KEYS)

        def do_g4(s, tg):
            region_barrier()
            load_wup(0, 0)
            for fc in range(NFC):
                slot = fc % 2
                if fc + 1 < NFC:
                    load_wup(fc + 1, (fc + 1) % 2)
                for gu in range(2):
                    ffn_up_chunk(fc, slot, gu)
                ffn_act(fc)
            if tg == 0:
                dbg("actT_%d" % s, actT, [128, NFC, 512], [("actT", fc) for fc in range(NFC)])
            for pair in range(2):
                ffn_down(s, tg, pair)
            region_barrier()

        for s in range(nseq):
            if stop < 1:
                break
            import os
            skip = os.environ.get("SKIPMS", "")
            if "0" not in skip: p.op("pool", lambda e: e.memset(kcTg[:, 0:16], 0.0), writes=["kcTg"])
            if "1" not in skip: p.op("pool", lambda e: e.memset(vcTg[:, 0:16], 0.0), writes=["vcTg"])
            if "2" not in skip: p.op("pool", lambda e: e.memset(kcmpT[:], 0.0), writes=["kcmpT"])
            if "3" not in skip: p.op("pool", lambda e: e.memset(vcmpT[:], 0.0), writes=["vcmpT"])
            if "4" not in skip: p.op("pool", lambda e: e.memset(Sst[:], 0.0), writes=["Sst"])
            if "5" not in skip: p.op("pool", lambda e: e.memset(S2_bf[:], 0.0), writes=["S2_lo", "S2_hi"])
            if "6" not in skip: p.op("pool", lambda e: e.memset(halo[:], 0.0), writes=[("halo", i) for i in range(2 * NFC)])
            for tg in range(4):
                if stop >= 2:
                    do_g1(s, tg)
                if stop >= 3:
                    do_g2(s, tg)
                for i4 in range(4):
                    if stop >= 4:
                        gla_core(s, tg, i4)
                    if stop >= 5:
                        for g in range(2):
                            nsa_group(s, tg, i4, g)
                    if stop >= 6:
                        out_proj(s, tg, i4)
                if tg == 0 and stop >= 6:
                    dbg("hres_%d" % s, hres[:], [128, 4, DM], [("hres", i, h_) for i in range(4) for h_ in range(2)])
                if stop >= 7:
                    do_g4(s, tg)
                if stop < 99:
                    break

        p.wait_all("sp", [o for e_ in ENG_NAMES for o in p.ops[e_] if o.dma])
        p.emit(nc, st)
    return nc, dbg_out


_CACHE = {}


def run_cores(inputs, nseq, dbg_names=(), stop=99):
    key = (nseq, tuple(dbg_names), stop)
    if key not in _CACHE:
        _CACHE[key] = build_program(nseq, dbg_names, stop)
    nc, dbg_out = _CACHE[key]
    consts = make_consts()
    x = np.ascontiguousarray(inputs["x"], dtype=np.float32)
    in_maps = []
    for c in range(NCORES):
        m = {"x": np.ascontiguousarray(x[c * nseq:(c + 1) * nseq].reshape(nseq * S, DM))}
        for k, shp in WEIGHT_SHAPES.items():
            m[k] = np.ascontiguousarray(np.asarray(inputs[k], dtype=np.float32).reshape(shp))
        for k in CONST_SHAPES:
            m["c_" + k] = consts[k]
        in_maps.append(m)
    res = run_bass_kernel_spmd(nc, in_maps, core_ids=list(range(NCORES)))
    return res


def kernel(**inputs):
    nseq = inputs["x"].shape[0] // NCORES
    res = run_cores(inputs, nseq)
    outs = [np.asarray(r["out"], dtype=np.float32).reshape(nseq, S, DM) for r in res.results]
    return np.concatenate(outs, axis=0)
```

```python
import contextlib
import numpy as np
import concourse.bass as bass
import concourse.mybir as mybir
from concourse.bass_utils import run_bass_kernel_spmd

F32 = mybir.dt.float32
BF16 = mybir.dt.bfloat16
AF = mybir.ActivationFunctionType
ALU = mybir.AluOpType
AX = mybir.AxisListType

S = 2048
DM = 1024
NT = 16
D_IN = 2856
DFF = 2816
NFC = 22
EPS = 1e-6
NCORES = 8
G1_MAXGI = [6]
LN8 = float(np.log(0.125))

ENG_NAMES = ("pe", "act", "dve", "pool", "sp")
PSUM_KEYS = {("HB", 0), ("HB", 1), ("HB", 2), ("HB", 3), ("SC", 0), ("SC", 1), "MK", "TR"}


class Op:
    __slots__ = ("eng", "fn", "idx", "dma", "deps", "signal", "token", "ndma", "dsem")

    def __init__(self, eng, fn, idx, dma):
        self.eng = eng
        self.fn = fn
        self.idx = idx
        self.dma = dma
        self.deps = []
        self.signal = False
        self.token = None
        self.ndma = 1
        self.dsem = None


class Prog:
    def __init__(self, n_dma_sems=8):
        self.ops = {e: [] for e in ENG_NAMES}
        self.last_write = {}
        self.readers = {}
        self.n_dma_sems = n_dma_sems
        self.dma_rr = {e: 0 for e in ENG_NAMES}
        self.dma_last = {}

    def op(self, eng, fn, reads=(), writes=(), dma=False, ndma=1):
        o = Op(eng, fn, len(self.ops[eng]), dma)
        o.ndma = ndma
        deps = {}
        ex = [k for k in reads if k in PSUM_KEYS]
        if ex:
            reads = [k for k in reads if k not in PSUM_KEYS]
            writes = list(writes) + ex
        for k in reads:
            w = self.last_write.get(k)
            if w is not None:
                deps[id(w)] = w
        for k in writes:
            w = self.last_write.get(k)
            if w is not None:
                deps[id(w)] = w
            for r in self.readers.get(k, ()):
                deps[id(r)] = r
        if dma:
            si = self.dma_rr[eng] % self.n_dma_sems
            self.dma_rr[eng] += 1
            o.dsem = (eng, si)
            prev = self.dma_last.get(o.dsem)
            if prev is not None:
                deps[id(prev)] = prev
            self.dma_last[o.dsem] = o
        deps.pop(id(o), None)
        o.deps = list(deps.values())
        for d in o.deps:
            d.signal = True
        for k in reads:
            self.readers.setdefault(k, []).append(o)
        for k in writes:
            self.last_write[k] = o
            self.readers[k] = []
        self.ops[eng].append(o)
        return o

    def wait_all(self, eng, ops):
        o = Op(eng, None, len(self.ops[eng]), False)
        o.deps = list(ops)
        for d in o.deps:
            d.signal = True
        self.ops[eng].append(o)
        return o

    def emit(self, nc, stack):
        esem = {e: stack.enter_context(nc.semaphore("s_" + e)) for e in ENG_NAMES}
        dsem = {}
        for e in ENG_NAMES:
            if any(o.dma for o in self.ops[e]):
                for i in range(self.n_dma_sems):
                    dsem[(e, i)] = stack.enter_context(nc.semaphore("d_%s%d" % (e, i)))
        dcount = {}
        for e in ENG_NAMES:
            c = 0
            for o in self.ops[e]:
                if o.fn is None:
                    continue
                if o.dma:
                    dcount[o.dsem] = dcount.get(o.dsem, 0) + 16 * o.ndma
                    o.token = (dsem[o.dsem], dcount[o.dsem])
                elif o.signal:
                    c += 1
                    o.token = (esem[e], c)
        ops = self.ops
        block = stack.enter_context(nc.Block())

        def run(e, engine):
            seen = {}
            for o in ops[e]:
                for d in o.deps:
                    if (not d.dma) and d.eng == e and e == "pe":
                        continue
                    sem, val = d.token
                    key = id(sem)
                    if seen.get(key, 0) >= val:
                        continue
                    seen[key] = val
                    engine.wait_ge(sem, val)
                if o.fn is None:
                    continue
                ins = o.fn(engine)
                if o.dma:
                    lst = ins if isinstance(ins, (list, tuple)) else [ins]
                    assert len(lst) == o.ndma, (len(lst), o.ndma)
                    for i_ in lst:
                        i_.then_inc(o.token[0], 16)
                elif o.signal:
                    ins.then_inc(o.token[0], 1)

        @block.tensor
        def _(engine):
            run("pe", engine)

        @block.scalar
        def _(engine):
            run("act", engine)

        @block.vector
        def _(engine):
            run("dve", engine)

        @block.gpsimd
        def _(engine):
            run("pool", engine)

        @block.sync
        def _(engine):
            run("sp", engine)


def make_consts():
    c = {}
    c["ident"] = np.eye(128, dtype=np.float32)
    k = np.arange(128)[:, None]
    q = np.arange(128)[None, :]
    c["cm"] = (k <= q).astype(np.float32)
    c["bm"] = (k > q).astype(np.float32)
    same = (k // 64) == (q // 64)
    c["tcum"] = (same & (k <= q)).astype(np.float32)
    c["tafter"] = (same & (k > q)).astype(np.float32)
    ci = np.zeros((128, 2), np.float32)
    ci[:64, 0] = 1.0
    ci[64:, 1] = 1.0
    c["chunkind"] = ci
    cc = np.arange(128)[:, None]
    pos = np.arange(S)[None, :]
    n = cc - 1
    c["cmpmask"] = ((cc >= 1) & (16 * n + 31 <= pos)).astype(np.float32)
    sj = np.arange(32)[None, :]
    ov = ((cc >= 1) & (n * 16 < (sj + 1) * 64) & (n * 16 + 32 > sj * 64)).astype(np.float32)
    c["overlap"] = ov
    p_ = np.arange(S)[:, None]
    cur = p_ // 64
    forced = (sj == 0) | (sj == cur) | (sj == cur - 1)
    causal = sj * 64 <= p_
    sb = np.where(causal, np.where(forced, 1e4, 0.0), -1e30).astype(np.float32)
    c["selbias"] = sb
    em = np.zeros((32, S), np.float32)
    em[np.arange(S) // 64, np.arange(S)] = 1.0
    c["emat"] = em
    half = 8
    inv = (500000.0 ** (-np.arange(half, dtype=np.float32) / half)).astype(np.float32)
    ang = np.arange(S, dtype=np.float32)[:, None] * inv[None, :]
    c["ropecs"] = np.concatenate([np.cos(ang), np.sin(ang)], axis=1).astype(np.float32)
    endp = (np.arange(128) - 1) * 16 + 31
    endp[0] = 0
    ange = endp.astype(np.float32)[:, None] * inv[None, :]
    c["ropecmp"] = np.concatenate([np.cos(ange), np.sin(ange)], axis=1).astype(np.float32)
    return c


CONST_SHAPES = {
    "ident": [128, 128], "cm": [128, 128], "bm": [128, 128], "tcum": [128, 128], "tafter": [128, 128],
    "chunkind": [128, 2], "cmpmask": [128, S], "overlap": [128, 32], "selbias": [S, 32],
    "emat": [32, S], "ropecs": [S, 16], "ropecmp": [128, 16],
}

WEIGHT_SHAPES = {
    "attn_norm": [1, DM], "w_in": [DM, D_IN], "nsa_q_norm": [1, 64], "nsa_k_norm": [3, 64],
    "cmp_pos_k": [32, 64], "cmp_pos_v": [32, 64], "cmp_w1_k": [2048, 256], "cmp_w2_k": [256, 64],
    "cmp_w1_v": [2048, 256], "cmp_w2_v": [256, 64], "gla_w_gate2": [16, 256], "gla_b_gate": [1, 256],
    "gla_out_norm": [1, 128], "w_out": [DM, DM], "ffn_norm": [1, DM], "w_up": [DM, 2 * DFF],
    "conv_w": [3, 2 * DFF], "conv_b": [1, 2 * DFF], "w_down": [DFF, DM],
}

WIN_PIECES = [
    (0, 0, 512),
    (512, 512, 256),
    (768, 896, 128),
    (896, 1152, 128),
    (1024, 768, 128),
    (1152, 1024, 128),
    (1280, 1280, 24),
    (1304, 2328, 16),
    (1320, 1304, 512),
    (1832, 1816, 512),
    (2344, 2344, 512),
]
WIN_GROUPS = [(0, 512), (512, 512), (1024, 296), (1320, 512), (1832, 512), (2344, 512)]


def bcast_rows(ap, nrows):
    n = ap.shape[-1]
    return bass.AP(ap.tensor, ap.offset, [[0, nrows], [1, n]])


def build_program(nseq, dbg_names=(), stop=99):
    nc = bass.Bass("TRN2", target_bir_lowering=False)
    D = {}
    D["x"] = nc.dram_tensor("x", [nseq * S, DM], F32, kind="ExternalInput").ap()
    for k, shp in WEIGHT_SHAPES.items():
        D[k] = nc.dram_tensor(k, shp, F32, kind="ExternalInput").ap()
    for k, shp in CONST_SHAPES.items():
        D[k] = nc.dram_tensor("c_" + k, shp, F32, kind="ExternalInput").ap()
    out = nc.dram_tensor("out", [nseq * S, DM], F32, kind="ExternalOutput").ap()
    win_bf = nc.dram_tensor("win_bf", [DM, D_IN], BF16, kind="Internal").ap()
    wout_bf = nc.dram_tensor("wout_bf", [DM, DM], BF16, kind="Internal").ap()
    wup_bf = nc.dram_tensor("wup_bf", [DM, 2 * DFF], BF16, kind="Internal").ap()
    wdn_bf = nc.dram_tensor("wdn_bf", [DFF, DM], BF16, kind="Internal").ap()
    w1k_bf = nc.dram_tensor("w1k_bf", [2048, 256], BF16, kind="Internal").ap()
    w1v_bf = nc.dram_tensor("w1v_bf", [2048, 256], BF16, kind="Internal").ap()
    dbg_out = {}

    p = Prog()
    with contextlib.ExitStack() as st:
        def sb(name, shape, dt=F32):
            return st.enter_context(nc.sbuf_tensor(name, shape, dt))

        def ps(name, shape, dt=F32):
            return st.enter_context(nc.psum_tensor(name, shape, dt))

        def dbg(name, tile_ap, shape, reads):
            if name not in dbg_names:
                return
            t = nc.dram_tensor("dbg_" + name, shape, tile_ap.dtype, kind="ExternalOutput").ap()
            dbg_out[name] = t
            p.op("sp", lambda e: e.dma_start(out=t, in_=tile_ap), reads=reads, dma=True)

        HB = ps("HB", [128, 4, 512])
        SC = ps("SC", [128, 2, 512])
        MK = ps("MK", [128, 512])
        TR = ps("TR", [128, 8, 128], BF16)

        ident = sb("ident", [128, 128], BF16)
        cm = sb("cm", [128, 128], BF16)
        bm = sb("bm", [128, 128], BF16)
        tcum_bf = sb("tcum_bf", [128, 128], BF16)
        tcum = sb("tcum", [128, 128])
        tafter = sb("tafter", [128, 128])
        chunkind = sb("chunkind", [128, 2])
        cmpmask = sb("cmpmask", [128, S], BF16)
        selbias = sb("selbias", [128, NT, 32])
        emat = sb("emat", [32, S], BF16)
        ropecs = sb("ropecs", [128, NT, 16])
        ropecmpg = sb("ropecmpg", [32, 4, 16])
        ones_row = sb("ones_row", [1, 128], BF16)
        ident_f = sb("ident_f", [128, 128])
        cst = sb("cst", [44, 4, 128])
        pest = sb("pest", [32, 2, 64], BF16)
        g1b = sb("g1b", [128, DM])
        g2b = sb("g2b", [128, DM])
        gq_b = sb("gq_b", [128, 8, 64])
        gk_b = sb("gk_b", [128, 4, 64])
        gkc_b = sb("gkc_b", [128, 2, 64])
        go_b = sb("go_b", [128, 4, 128])
        wg2 = sb("wg2", [16, 256], BF16)
        bgate = sb("bgate", [1, 256], BF16)
        w2k = sb("w2k", [128, 2, 64], BF16)
        w2v = sb("w2v", [128, 2, 64], BF16)
        peT = sb("peT", [64, 2, 32], BF16)
        c1 = sb("c1", [128, 2, 2])
        convw = sb("convw", [128, 3, 2 * NFC])
        convb = sb("convb", [128, 2 * NFC])
        woutc = [sb("woutc%d" % i, [128, DM], BF16) for i in range(2)]
        dummy = sb("dummy_t", [1, 8])

        xt = [sb("xt%d" % i, [128, DM]) for i in range(2)]
        junk = sb("junk", [128, DM], BF16)
        st4 = sb("st4", [128, 16])
        xn = sb("xn", [128, DM], BF16)
        RA = sb("RA", [128, 6144])
        RB = sb("RB", [128, 6400])
        wing = [RA[:, i * 2048:(i + 1) * 2048].bitcast(BF16).rearrange("p (k c) -> p k c", k=8) for i in range(2)]
        xnT = RA[:, 4096:6144].bitcast(BF16).rearrange("p (a k c) -> p a k c", a=4, k=8)
        actT = RA[:, 0:5632].bitcast(BF16).rearrange("p (f c) -> p f c", f=NFC)
        eB = RB[:, 0:3072].rearrange("p (a b c) -> p a b c", a=4, b=3)
        sgo = RB[:, 3072:5120].rearrange("p (a c) -> p a c", a=4)
        gvb = RB[:, 5120:6144].bitcast(BF16).rearrange("p (a c) -> p a c", a=4)
        ubuf = [[RB[:, (i * 2 + j) * 514:(i * 2 + j + 1) * 514] for j in range(2)] for i in range(2)]
        ybuf = [RB[:, 2056 + j * 512:2056 + (j + 1) * 512] for j in range(2)]
        wupc = [RB[:, 3080 + i * 1024:3080 + (i + 1) * 1024].bitcast(BF16).rearrange("p (k c) -> p k c", k=8) for i in range(2)]
        wdnc = [RB[:, 5128 + i * 512:5128 + (i + 1) * 512].bitcast(BF16) for i in range(2)]
        sq = sb("sq", [128, 512])
        qn = sb("qn", [128, 512])
        ropeA = sb("ropeA", [128, 8, 16])
        ropeB = sb("ropeB", [128, 8, 16])
        qb = sb("qb", [128, 4, 2, 64], BF16)
        qTg = sb("qTg", [128, 4, 4, 128], BF16)
        kb = sb("kb", [128, 4, 64], BF16)
        kvb = sb("kvb", [128, 256], BF16)
        ksT = sb("ksT", [128, S], BF16)
        kwT = sb("kwT", [128, S], BF16)
        vsx = sb("vsx", [128, NT, 2, 65], BF16)
        vwx = sb("vwx", [128, NT, 2, 65], BF16)
        kcTg = sb("kcTg", [128, 544], BF16)
        vcTg = sb("vcTg", [128, 544], BF16)
        sigg = sb("sigg", [128, 4, 24])
        w1c = [sb("w1c%d" % i, [128, 8, 256], BF16) for i in range(2)]
        hTc = sb("hTc", [128, 2, 2, 2, 32], BF16)
        kcn = sb("kcn", [32, 2, 64])
        kcb = sb("kcb", [32, 2, 64], BF16)
        kcmpT = sb("kcmpT", [128, 128], BF16)
        vcmpT = sb("vcmpT", [64, 2, 128], BF16)
        VcX = sb("VcX", [128, 2, 97], BF16)
        glrb = sb("glrb", [128, 16], BF16)
        glrT = sb("glrT", [16, 128], BF16)
        e1 = sb("e1", [128, 256])
        lsp2 = sb("lsp2", [128, 4, 2, 64])
        decayg = sb("decayg", [128, 4, 4, 2])
        qtb2 = sb("qtb2", [128, 4, 2, 64], BF16)
        ktb2 = sb("ktb2", [128, 4, 2, 64], BF16)
        kdec2 = sb("kdec2", [128, 4, 4, 2, 64], BF16)
        qkT = sb("qkT", [128, 4, 8, 128], BF16)
        Sst = sb("Sst", [128, 4, 128])
        S2_bf = sb("S2_bf", [128, 4, 128], BF16)
        ATm = sb("ATm", [128, 4, 128], BF16)
        og = sb("og", [128, 4, 128])
        obf = sb("obf", [128, DM], BF16)
        oT = sb("oT", [128, 8, 128], BF16)
        Pb = [sb("Pb%d" % i, [128, 4, 128], BF16) for i in range(3)]
        score = sb("score", [128, 32])
        top8 = sb("top8", [128, 8])
        selm = sb("selm", [128, 32], BF16)
        selT = sb("selT", [32, 128], BF16)
        rinv = sb("rinv", [128, 3, 4])
        gf = sb("gf", [128, 3, 4])
        oacc = sb("oacc", [128, 4, 64])
        otmp = sb("otmp", [128, 4, 64])
        hres = sb("hres", [128, 4, DM])
        hn = sb("hn", [128, DM], BF16)
        hnT = sb("hnT", [128, 8, 512], BF16)
        halo = sb("halo", [128, 2 * NFC, 2])

        for (dst, src, w) in WIN_PIECES:
            p.op("pool", (lambda dst, src, w: lambda e: e.dma_start(out=win_bf[:, dst:dst + w], in_=D["w_in"][:, src:src + w]))(dst, src, w),
                 writes=[("win_bf", dst)], dma=True)
        for r in range(8):
            p.op("pool", (lambda r: lambda e: e.dma_start(out=wout_bf[r * 128:(r + 1) * 128, :], in_=D["w_out"][r * 128:(r + 1) * 128, :]))(r),
                 writes=[("wout_bf", r)], dma=True)
        for nm, dst_, src_ in (("w1k", w1k_bf, D["cmp_w1_k"]), ("w1v", w1v_bf, D["cmp_w1_v"])):
            for r in range(4):
                p.op("pool", (lambda r, dst_, src_: lambda e: e.dma_start(out=dst_[r * 512:(r + 1) * 512, :], in_=src_[r * 512:(r + 1) * 512, :]))(r, dst_, src_),
                     writes=[(nm, r)], dma=True)

        def cload(eng, dst_ap, src_ap, key):
            p.op(eng, lambda e: e.dma_start(out=dst_ap, in_=src_ap), writes=[key], dma=True)

        cload("pool", ident[:], D["ident"], "ident")
        cload("pool", cm[:], D["cm"], "cm")
        cload("pool", bm[:], D["bm"], "bm")
        cload("pool", tcum_bf[:], D["tcum"], "tcum_bf")
        cload("sp", tcum[:], D["tcum"], "tcum")
        cload("sp", tafter[:], D["tafter"], "tafter")
        cload("sp", chunkind[:], D["chunkind"], "chunkind")
        cload("pool", cmpmask[:], D["cmpmask"], "cmpmask")
        cload("sp", selbias[:], D["selbias"].rearrange("(t p) j -> p t j", p=128), "selbias")
        cload("pool", emat[:], D["emat"], "emat")
        cload("sp", ropecs[:], D["ropecs"].rearrange("(t p) j -> p t j", p=128), "ropecs")
        cload("sp", ropecmpg[:], D["ropecmp"].rearrange("(t p) j -> p t j", p=32), "ropecmpg")
        cload("sp", g1b[:], bcast_rows(D["attn_norm"], 128), "g1b")
        cload("sp", g2b[:], bcast_rows(D["ffn_norm"], 128), "g2b")
        for h in range(8):
            cload("sp", gq_b[:, h, :], bcast_rows(D["nsa_q_norm"], 128), "gq_b")
        for j, r in enumerate((1, 1, 2, 2)):
            cload("sp", gk_b[:, j, :], bcast_rows(D["nsa_k_norm"][r:r + 1, :], 128), "gk_b")
        for j in range(2):
            cload("sp", gkc_b[:, j, :], bcast_rows(D["nsa_k_norm"][0:1, :], 128), "gkc_b")
        for h in range(4):
            cload("sp", go_b[:, h, :], bcast_rows(D["gla_out_norm"], 128), "go_b")
        cload("pool", wg2[:], D["gla_w_gate2"], "wg2")
        cload("pool", bgate[:], D["gla_b_gate"], "bgate")
        cload("pool", w2k[:], D["cmp_w2_k"].rearrange("(a p) d -> p a d", p=128), "w2k")
        cload("pool", w2v[:], D["cmp_w2_v"].rearrange("(a p) d -> p a d", p=128), "w2v")
        cload("sp", ident_f[:], D["ident"], "ident_f")
        cload("pool", pest[:, 0, :], D["cmp_pos_k"], "pest0")
        cload("pool", pest[:, 1, :], D["cmp_pos_v"], "pest1")

        def f(e):
            e.transpose(TR[0:64, 0, 0:32], pest[:, 0, :], ident[0:32, 0:32])
            return e.transpose(TR[0:64, 1, 0:32], pest[:, 1, :], ident[0:32, 0:32])
        p.op("pe", f, reads=["pest0", "pest1", "ident"], writes=["TR"])
        p.op("act", lambda e: e.copy(peT[:], TR[0:64, 0:2, 0:32]), reads=["TR"], writes=["peTk", "peTv"])
        for k_ in range(3):
            cload("sp", cst[:, k_, :], D["conv_w"][k_:k_ + 1, :].rearrange("o (c p) -> (o c) p", p=128), ("cst", k_))
        cload("sp", cst[:, 3, :], D["conv_b"].rearrange("o (c p) -> (o c) p", p=128), ("cst", 3))

        def f(e):
            ins = None
            for k_ in range(4):
                ins = e.transpose(MK[:, k_ * 64:k_ * 64 + 44], cst[:, k_, :], ident_f[0:44, 0:44])
            return ins
        p.op("pe", f, reads=[("cst", k_) for k_ in range(4)] + ["ident_f"], writes=["MK"])
        p.op("act", lambda e: e.copy(convw[:], MK[:, 0:192].rearrange("p (k c) -> p k c", c=64)[:, :, 0:44]), reads=["MK"], writes=["convw"])
        p.op("act", lambda e: e.copy(convb[:], MK[:, 192:236]), reads=["MK"], writes=["convb"])
        p.op("dve", lambda e: e.memset(ones_row[:], 1.0), writes=["ones_row"])
        p.op("dve", lambda e: e.memset(vsx[:], 1.0), writes=["vsx_init"])
        p.op("dve", lambda e: e.memset(vwx[:], 1.0), writes=["vwx_init"])
        p.op("dve", lambda e: e.memset(VcX[:], 1.0), writes=["VcX_init"])
        for g in range(2):
            p.op("pool", (lambda g: lambda e: e.dma_start(out=VcX[:, g, 65:97], in_=D["overlap"]))(g),
                 reads=["VcX_init"], writes=[("VcX_ov", g)], dma=True)
        for r in range(8):
            p.op("pool", (lambda r: lambda e: e.dma_start(out=wup_bf[r * 128:(r + 1) * 128, :], in_=D["w_up"][r * 128:(r + 1) * 128, :]))(r),
                 writes=[("wup_bf", r)], dma=True)
        for r in range(NFC):
            p.op("pool", (lambda r: lambda e: e.dma_start(out=wdn_bf[r * 128:(r + 1) * 128, :], in_=D["w_down"][r * 128:(r + 1) * 128, :]))(r),
                 writes=[("wdn_bf", r)], dma=True)
        WUP_KEYS = [("wup_bf", r) for r in range(8)]

        w1_state = {"n": 0}

        def load_w1_chunk(kv, ch):
            i = w1_state["n"] % 2
            w1_state["n"] += 1
            src = (w1k_bf, w1v_bf)[kv]
            nm = ("w1k", "w1v")[kv]
            view = src[ch * 512:(ch + 1) * 512, :].rearrange("(l d) h -> d l h", d=64)

            def f(e, i=i, view=view):
                a = e.dma_start(out=w1c[i][0:64, :, :], in_=view)
                b = e.dma_start(out=w1c[i][64:128, :, :], in_=view)
                return [a, b]
            p.op("sp", f, reads=[(nm, ch)], writes=[("w1c", i)], dma=True, ndma=2)
            return i

        def c1_chunk(kv, ch, i):
            def f(e):
                ins = None
                for ll in range(8):
                    l = ch * 8 + ll
                    for half in range(2):
                        ins = e.matmul(HB[:, half, 0:1], w1c[i][0:64, ll, half * 128:(half + 1) * 128],
                                       peT[:, kv, l:l + 1], start=(l == 0), stop=(l == 31))
                return ins
            p.op("pe", f, reads=[("w1c", i), "peTk", "peTv"], writes=[("HB", 0), ("HB", 1)])

        for kv in range(2):
            for ch in range(4):
                i = load_w1_chunk(kv, ch)
                c1_chunk(kv, ch, i)
            p.op("act", (lambda kv: lambda e: e.copy(c1[:, kv, :], HB[:, 0:2, 0]))(kv),
                 reads=[("HB", 0), ("HB", 1)], writes=[("c1", kv)])

        p.wait_all("pool", [o for o in p.ops["pool"] if o.dma])

        def rstd_from_ss(ap, scale, key):
            p.op("act", lambda e: e.activation(ap, ap, AF.Sqrt, bias=EPS, scale=scale), reads=[key], writes=[key])
            p.op("dve", lambda e: e.reciprocal(ap, ap), reads=[key], writes=[key])

        x_state = {"n": 0}

        def load_x(s, tt):
            i = x_state["n"] % 2
            x_state["n"] += 1
            r0 = s * S + tt * 128
            p.op("sp", lambda e: e.dma_start(out=xt[i][:], in_=D["x"][r0:r0 + 128, :]), writes=[("xt", i)], dma=True)
            return i

        wing_state = {"n": 0}
        WIN_KEYS = [("win_bf", d_) for (d_, _, _) in WIN_PIECES]

        def load_wing(gi):
            i = wing_state["n"] % 2
            wing_state["n"] += 1
            c0, w = WIN_GROUPS[gi]
            p.op("sp", lambda e: e.dma_start(out=wing[i][:, :, 0:w], in_=win_bf[:, c0:c0 + w].rearrange("(k p) c -> p k c", p=128)),
                 reads=WIN_KEYS, writes=[("wing", i)], dma=True)
            return i

        def norm_rope(src_ap, nh, gain_ap, gain_key, pre_scale, cs_ap, cs_key, src_keys, P=128):
            sq_v = sq[0:P, 0:nh * 64].rearrange("p (h d) -> p h d", d=64)
            qn_v = qn[0:P, 0:nh * 64].rearrange("p (h d) -> p h d", d=64)
            ss = st4[0:P, 0:nh]
            p.op("act", lambda e: e.activation(sq_v, src_ap, AF.Square), reads=src_keys, writes=["sq"])
            p.op("dve", lambda e: e.tensor_reduce(ss, sq_v, AX.X, ALU.add), reads=["sq"], writes=["st4"])
            rstd_from_ss(ss, 1.0 / 64.0, "st4")
            p.op("dve", lambda e: e.scalar_tensor_tensor(qn_v, src_ap, float(pre_scale), gain_ap, ALU.mult, ALU.mult),
                 reads=list(src_keys) + [gain_key], writes=["qn"])
            p.op("dve", lambda e: e.tensor_tensor(qn_v, qn_v, ss.unsqueeze(2).broadcast_to([P, nh, 64]), ALU.mult),
                 reads=["qn", "st4"], writes=["qn"])
            cosb = cs_ap[:, 0:8].unsqueeze(1).unsqueeze(1).broadcast_to([P, nh, 2, 8])
            sinb = cs_ap[:, 8:16].unsqueeze(1).broadcast_to([P, nh, 8])
            rA = ropeA[0:P, 0:nh, :]
            rB = ropeB[0:P, 0:nh, :]
            p.op("pool", lambda e: e.tensor_tensor(rA.rearrange("p h (t d) -> p h t d", t=2), qn_v[:, :, 0:16].rearrange("p h (t d) -> p h t d", t=2), cosb, ALU.mult),
                 reads=["qn", cs_key], writes=["ropeA"])
            p.op("pool", lambda e: e.tensor_tensor(rB[:, :, 0:8], qn_v[:, :, 8:16], sinb, ALU.mult), reads=["qn", cs_key], writes=["ropeB0"])
            p.op("pool", lambda e: e.tensor_tensor(rB[:, :, 8:16], qn_v[:, :, 0:8], sinb, ALU.mult), reads=["qn", cs_key], writes=["ropeB1"])
            p.op("pool", lambda e: e.tensor_tensor(qn_v[:, :, 0:8], rA[:, :, 0:8], rB[:, :, 0:8], ALU.subtract),
                 reads=["ropeA", "ropeB0", "ropeB1"], writes=["qn"])
            p.op("pool", lambda e: e.tensor_tensor(qn_v[:, :, 8:16], rA[:, :, 8:16], rB[:, :, 8:16], ALU.add),
                 reads=["ropeA", "ropeB0", "ropeB1"], writes=["qn"])
            return qn_v

        tr_state = {"n": 0}

        def tr_half():
            i = tr_state["n"] % 2
            tr_state["n"] += 1
            return i

        def transposes(srcs, src_keys, P_out=128, idn=None):
            th = tr_half()
            idn_ = ident[:] if idn is None else idn

            def f(e):
                ins = None
                for j, a in enumerate(srcs):
                    m = a.shape[1]
                    kk = a.shape[0]
                    ins = e.transpose(TR[0:m, th * 4 + j, 0:kk], a, idn_[0:kk, 0:kk])
                return ins
            p.op("pe", f, reads=list(src_keys) + ["ident"], writes=["TR"])
            return th

        HK = [("HB", b) for b in range(4)]

        def g1_xnorm(s, tg, i4):
            import os
            parts = os.environ.get("G1PARTS", "xsrntc")
            tt = tg * 4 + i4
            xi = load_x(s, tt) if "x" in parts else 0
            if "s" in parts:
                p.op("act", lambda e: e.activation(junk[:], xt[xi][:], AF.Square, accum_out=st4[:, 8:9]),
                     reads=[("xt", xi)], writes=["junk", "st4x"])
            if "r" in parts:
                rstd_from_ss(st4[:, 8:9], 1.0 / DM, "st4x")
            if "n" in parts:
                p.op("dve", lambda e: e.scalar_tensor_tensor(xn[:], xt[xi][:], st4[:, 8:9], g1b[:], ALU.mult, ALU.mult),
                     reads=[("xt", xi), "st4x", "g1b"], writes=["xn"])
            for hh in range(2):
                if "t" in parts:
                    th = transposes([xn[:, (hh * 4 + k) * 128:(hh * 4 + k + 1) * 128] for k in range(4)], ["xn"])
                else:
                    th = 0
                if "c" in parts:
                    p.op("act", (lambda hh, th: lambda e: e.copy(xnT[:, i4, hh * 4:(hh + 1) * 4, :], TR[:, th * 4:(th + 1) * 4, :]))(hh, th),
                         reads=["TR"], writes=[("xnT", i4, hh)])

        def g1_z(s, tg, gi, i4, wi):
            tt = tg * 4 + i4
            c0, w = WIN_GROUPS[gi]
            zb = i4
            Z = HB[:, zb, :]
            ZK = [("HB", zb)]

            def f(e):
                ins = None
                for k in range(8):
                    ins = e.matmul(HB[:, zb, 0:w], xnT[:, i4, k, :], wing[wi][:, k, 0:w], start=(k == 0), stop=(k == 7))
                return ins
            p.op("pe", f, reads=[("wing", wi), ("xnT", i4, 0), ("xnT", i4, 1)], writes=ZK)
            if gi == 0:
                qn_v = norm_rope(Z.rearrange("p (h d) -> p h d", d=64), 8, gq_b[:], "gq_b", 0.125, ropecs[:, tt, :], "ropecs", ZK)
                p.op("act", lambda e: e.copy(qb[:].rearrange("p h g d -> p g h d"), qn[:, 0:512].rearrange("p (g h d) -> p g h d", g=2, h=4)),
                     reads=["qn"], writes=["qb"])
                th = transposes([qb[:, h, :, :].rearrange("p g d -> p (g d)") for h in range(4)], ["qb"])
                p.op("act", lambda e: e.copy(qTg[:, i4, :, :], TR[:, th * 4:(th + 1) * 4, :]), reads=["TR"], writes=[("qTg", i4)])
            elif gi == 1:
                p.op("act", lambda e: e.copy(kvb[:], Z[:, 0:256]), reads=ZK, writes=["kvb"])
                th = transposes([kvb[:, 0:128], kvb[:, 128:256]], ["kvb"])
                c_ = 16 + i4 * 128
                p.op("act", lambda e: e.copy(kcTg[:, c_:c_ + 128], TR[:, th * 4 + 0, :]), reads=["TR"], writes=["kcTg"])
                p.op("dve", lambda e: e.tensor_copy(vcTg[:, c_:c_ + 128], TR[:, th * 4 + 1, :]), reads=["TR"], writes=["vcTg"])
                p.op("act", lambda e: e.copy(vsx[:, tt, :, 0:64], Z[:, 256:384].rearrange("p (g d) -> p g d", d=64)),
                     reads=ZK + ["vsx_init"], writes=[("vsx", tt)])
                p.op("dve", lambda e: e.tensor_copy(vwx[:, tt, :, 0:64], Z[:, 384:512].rearrange("p (g d) -> p g d", d=64)),
                     reads=ZK + ["vwx_init"], writes=[("vwx", tt)])
            elif gi == 2:
                qn_v = norm_rope(Z[:, 0:256].rearrange("p (h d) -> p h d", d=64), 4, gk_b[:], "gk_b", 1.0, ropecs[:, tt, :], "ropecs", ZK)
                p.op("act", lambda e: e.copy(kb[:], qn_v), reads=["qn"], writes=["kb"])
                th = transposes([kb[:, 0:2, :].rearrange("p g d -> p (g d)"), kb[:, 2:4, :].rearrange("p g d -> p (g d)")], ["kb"])
                p.op("act", lambda e: e.copy(ksT[:, tt * 128:(tt + 1) * 128], TR[:, th * 4 + 0, :]), reads=["TR"], writes=[("ksT", tt)])
                p.op("dve", lambda e: e.tensor_copy(kwT[:, tt * 128:(tt + 1) * 128], TR[:, th * 4 + 1, :]), reads=["TR"], writes=[("kwT", tt)])
                p.op("act", lambda e: e.activation(sigg[:, i4, :], Z[:, 256:280], AF.Sigmoid), reads=ZK, writes=[("sigg", i4)])
                p.op("act", lambda e: e.copy(glrb[:], Z[:, 280:296]), reads=ZK, writes=["glrb"])
                th2 = transposes([glrb[:]], ["glrb"])
                p.op("act", lambda e: e.copy(glrT[:], TR[0:16, th2 * 4, :]), reads=["TR"], writes=["glrT"])

                def f(e):
                    e.matmul(MK[:, 0:256], glrT[:], wg2[:], start=True, stop=False)
                    return e.matmul(MK[:, 0:256], ones_row[:], bgate[:], start=False, stop=True)
                p.op("pe", f, reads=["glrT", "wg2", "bgate", "ones_row"], writes=["MK"])
                p.op("act", lambda e: e.activation(e1[:], MK[:, 0:256], AF.Exp, scale=-1.0), reads=["MK"], writes=["e1"])
                for c in range(2):
                    p.op("act", (lambda c: lambda e: e.activation(lsp2[:, :, c, :], e1[:].rearrange("p (h d) -> p h d", d=64), AF.Ln, bias=1.0))(c),
                         reads=["e1"], writes=[("lsp2", c)])
                LS = [("lsp2", 0), ("lsp2", 1)]

                def f(e):
                    e.matmul(MK[:, 0:256].rearrange("p (h d) -> p h d", d=64), tcum[:], lsp2[:, :, 0, :], start=True, stop=True)
                    return e.matmul(MK[:, 256:512].rearrange("p (h d) -> p h d", d=64), tafter[:], lsp2[:, :, 0, :], start=True, stop=True)
                p.op("pe", f, reads=LS + ["tcum", "tafter"], writes=["MK"])
                p.op("act", lambda e: e.activation(eB[:, i4, 0, :], MK[:, 0:256], AF.Exp, scale=-1.0 / 16.0, bias=LN8), reads=["MK"], writes=[("eB", i4, 0)])
                p.op("act", lambda e: e.activation(eB[:, i4, 1, :], MK[:, 0:256], AF.Exp, scale=1.0 / 16.0), reads=["MK"], writes=[("eB", i4, 1)])
                p.op("act", lambda e: e.activation(eB[:, i4, 2, :], MK[:, 256:512], AF.Exp, scale=-1.0 / 16.0), reads=["MK"], writes=[("eB", i4, 2)])

                def f(e):
                    ins = None
                    for h in range(4):
                        ins = e.matmul(MK[:, 2 * h:2 * h + 2], lsp2[:, h, :, :].rearrange("p c d -> p (c d)"), chunkind[:], start=True, stop=True)
                    return ins
                p.op("pe", f, reads=LS + ["chunkind"], writes=["MK"])
                p.op("act", lambda e: e.activation(decayg[:, i4, :, :].rearrange("p h c -> p (h c)"), MK[:, 0:8], AF.Exp, scale=-1.0 / 16.0),
                     reads=["MK"], writes=[("decayg", i4)])
            elif gi == 3:
                def v3(ap):
                    return ap.rearrange("p (h d) -> p h d", d=64)
                for c in range(2):
                    p.op("dve", (lambda c: lambda e: e.scalar_tensor_tensor(qtb2[:, :, c, :], v3(Z[:, 0:256]), chunkind[:, c:c + 1], v3(eB[:, i4, 0, :]), ALU.mult, ALU.mult))(c),
                         reads=ZK + [("eB", i4, 0), "chunkind"], writes=[("qtb2", c)])
                    p.op("dve", (lambda c: lambda e: e.tensor_tensor(ktb2[:, :, c, :], v3(Z[:, 256:512]), v3(eB[:, i4, 1, :]), ALU.mult))(c),
                         reads=ZK + [("eB", i4, 1)], writes=[("ktb2", c)])
                    p.op("dve", (lambda c: lambda e: e.tensor_tensor(kdec2[:, i4, :, c, :], v3(Z[:, 256:512]), v3(eB[:, i4, 2, :]), ALU.mult))(c),
                         reads=ZK + [("eB", i4, 2)], writes=[("kdec2", i4, c)])
                for qk, srcb, key in ((0, qtb2, "qtb2"), (1, ktb2, "ktb2")):
                    th = transposes([srcb[:, h, :, :].rearrange("p c d -> p (c d)") for h in range(4)], [(key, 0), (key, 1)])
                    p.op("act", (lambda th, qk: lambda e: e.copy(qkT[:, i4, qk * 4:(qk + 1) * 4, :], TR[:, th * 4:(th + 1) * 4, :]))(th, qk),
                         reads=["TR"], writes=[("qkT", i4, qk)])
            elif gi == 4:
                p.op("act", lambda e: e.copy(gvb[:, i4, :], Z), reads=ZK, writes=[("gvb", i4)])
            elif gi == 5:
                p.op("act", lambda e: e.activation(sgo[:, i4, :], Z, AF.Silu), reads=ZK, writes=[("sgo", i4)])
                p.op("pool", lambda e: e.tensor_tensor(sgo[:, i4, :].rearrange("p (h v) -> p h v", v=128), sgo[:, i4, :].rearrange("p (h v) -> p h v", v=128), go_b[:], ALU.mult),
                     reads=[("sgo", i4), "go_b"], writes=[("sgo", i4)])

        def do_g1(s, tg):
            import os
            wi = load_wing(0) if "w" in os.environ.get("G1PARTS", "w") else 0
            for i4 in range(4):
                g1_xnorm(s, tg, i4)
            for gi in range(min(6, G1_MAXGI[0])):
                wi_next = load_wing(gi + 1) if gi < 5 else None
                for i4 in range(4):
                    g1_z(s, tg, gi, i4, wi)
                wi = wi_next

        def g2_first(kv, ch, wi1):
            srcT = (kcTg, vcTg)[kv]
            skey = ("kcTg", "vcTg")[kv]

            def f(e):
                ins = None
                for ll in range(8):
                    l = ch * 8 + ll
                    for g in range(2):
                        rhs = srcT[64 * g:64 * g + 64, l:l + 512].rearrange("p (n s) -> p n s", s=16)[:, :, 0]
                        for half in range(2):
                            ins = e.matmul(HB[:, g * 2 + half, 0:32], w1c[wi1][64 * g:64 * g + 64, ll, half * 128:(half + 1) * 128],
                                           rhs, start=(l == 0), stop=(l == 31))
                return ins
            p.op("pe", f, reads=[("w1c", wi1), skey], writes=HK)

        def g2_gelu(kv, g, half):
            p.op("act", lambda e: e.activation(hTc[:, kv, g, half, :], HB[:, g * 2 + half, 0:32], AF.Gelu_apprx_tanh, bias=c1[:, kv, half:half + 1]),
                 reads=[("HB", g * 2 + half), ("c1", kv)], writes=[("hTc", kv, g, half)])

        def do_g2(s, tg):
            for kv in range(2):
                for ch in range(4):
                    wi1 = load_w1_chunk(kv, ch)
                    g2_first(kv, ch, wi1)
                for g in range(2):
                    for half in range(2):
                        g2_gelu(kv, g, half)
            p.op("pool", lambda e: e.tensor_copy(kcTg[:, 0:16], kcTg[:, 512:528]), reads=["kcTg"], writes=["kcTg"])
            p.op("pool", lambda e: e.tensor_copy(vcTg[:, 0:16], vcTg[:, 512:528]), reads=["vcTg"], writes=["vcTg"])

            def f(e):
                ins = None
                for g in range(2):
                    for half in range(2):
                        ins = e.matmul(MK[0:32, g * 64:(g + 1) * 64], hTc[:, 0, g, half, :], w2k[:, half, :], start=(half == 0), stop=(half == 1))
                for g in range(2):
                    for half in range(2):
                        ins = e.matmul(MK[0:64, 128 + g * 32:128 + (g + 1) * 32], w2v[:, half, :], hTc[:, 1, g, half, :], start=(half == 0), stop=(half == 1))
                return ins
            p.op("pe", f, reads=[("hTc", kv, g, half) for kv in range(2) for g in range(2) for half in range(2)] + ["w2k", "w2v"], writes=["MK"])
            p.op("act", lambda e: e.copy(vcmpT[:, :, tg * 32:(tg + 1) * 32], MK[0:64, 128:192].rearrange("p (g n) -> p g n", n=32)),
                 reads=["MK"], writes=["vcmpT"])
            qn_v = norm_rope(MK[0:32, 0:128].rearrange("p (g d) -> p g d", d=64), 2, gkc_b[0:32], "gkc_b", 1.0, ropecmpg[:, tg, :], "ropecmpg", ["MK"], P=32)
            p.op("act", lambda e: e.copy(kcb[:], qn_v), reads=["qn"], writes=["kcb"])
            th = transposes([kcb[:].rearrange("p g d -> p (g d)")], ["kcb"])
            p.op("act", lambda e: e.copy(kcmpT[:, tg * 32:(tg + 1) * 32], TR[:, th * 4, 0:32]), reads=["TR"], writes=["kcmpT"])
            th2 = transposes([vcmpT[:, 0, :], vcmpT[:, 1, :]], ["vcmpT"])
            p.op("act", lambda e: e.copy(VcX[:, :, 0:64], TR[:, th2 * 4:th2 * 4 + 2, 0:64]),
                 reads=["TR", "VcX_init", ("VcX_ov", 0), ("VcX_ov", 1)], writes=["VcX"])
            dbg("qTg_%d_%d" % (s, tg), qTg[:], [128, 4, 4, 128], [("qTg", i) for i in range(4)])
            dbg("kcmpT_%d_%d" % (s, tg), kcmpT[:], [128, 128], ["kcmpT"])
            dbg("VcX_%d_%d" % (s, tg), VcX[:], [128, 2, 97], ["VcX"])
            dbg("ksT_%d_%d" % (s, tg), ksT[:], [128, S], [("ksT", t_) for t_ in range(tg * 4 + 4)])
            dbg("vsx_%d_%d" % (s, tg), vsx[:], [128, NT, 2, 65], [("vsx", t_) for t_ in range(tg * 4 + 4)])
            dbg("sgo_%d_%d" % (s, tg), sgo, [128, 4, 512], [("sgo", i) for i in range(4)])
            dbg("qkT_%d_%d" % (s, tg), qkT[:], [128, 4, 8, 128], [("qkT", i, j) for i in range(4) for j in range(2)])
            dbg("decayg_%d_%d" % (s, tg), decayg[:], [128, 4, 4, 2], [("decayg", i) for i in range(4)])

        def gla_core(s, tg, i4):
            tt = tg * 4 + i4

            def f(e):
                ins = None
                for h in range(4):
                    ins = e.matmul(SC[:, 0, h * 128:(h + 1) * 128], qkT[:, i4, 4 + h, :], qkT[:, i4, h, :], start=True, stop=True)
                return ins
            p.op("pe", f, reads=[("qkT", i4, 0), ("qkT", i4, 1)], writes=[("SC", 0)])
            p.op("dve", lambda e: e.tensor_tensor(ATm[:], SC[:, 0, :].rearrange("p (h i) -> p h i", i=128),
                                                  tcum_bf[:].unsqueeze(1).broadcast_to([128, 4, 128]), ALU.mult),
                 reads=[("SC", 0), "tcum_bf"], writes=["ATm"])

            def dS(c):
                def f(e):
                    ins = None
                    for h in range(4):
                        ins = e.matmul(MK[:, h * 128:(h + 1) * 128], kdec2[64 * c:64 * c + 64, i4, h, :, :].rearrange("p c d -> p (c d)"),
                                       gvb[64 * c:64 * c + 64, i4, 128 * h:128 * h + 128], start=True, stop=True)
                    return ins
                p.op("pe", f, reads=[("kdec2", i4, 0), ("kdec2", i4, 1), ("gvb", i4)], writes=["MK"])

            def state_update(c):
                dec = decayg[:, i4, :, c:c + 1].broadcast_to([128, 4, 128])
                p.op("dve", lambda e: e.tensor_tensor(Sst[:], Sst[:], dec, ALU.mult), reads=["Sst", ("decayg", i4)], writes=["Sst"])
                p.op("dve", lambda e: e.tensor_tensor(Sst[:], Sst[:], MK[:, :].rearrange("p (h v) -> p h v", v=128), ALU.add),
                     reads=["Sst", "MK"], writes=["Sst"])

            dS(0)
            state_update(0)
            p.op("act", lambda e: e.copy(S2_bf[64:128, :, :], Sst[64:128, :, :]), reads=["Sst"], writes=["S2_hi"])

            def f(e):
                ins = None
                for h in range(4):
                    e.matmul(SC[:, 1, h * 128:(h + 1) * 128], ATm[:, h, :], gvb[:, i4, 128 * h:128 * h + 128], start=True, stop=False)
                    ins = e.matmul(SC[:, 1, h * 128:(h + 1) * 128], qkT[:, i4, h, :], S2_bf[:, h, :], start=False, stop=True)
                return ins
            p.op("pe", f, reads=["ATm", ("gvb", i4), ("qkT", i4, 0), "S2_lo", "S2_hi"], writes=[("SC", 1)])
            dS(1)
            state_update(1)
            p.op("act", lambda e: e.copy(S2_bf[0:64, :, :], Sst[0:64, :, :]), reads=["Sst"], writes=["S2_lo"])
            O3 = SC[:, 1, :].rearrange("p (h v) -> p h v", v=128)
            p.op("act", lambda e: e.activation(sq[:], SC[:, 1, :], AF.Square), reads=[("SC", 1)], writes=["sq"])
            p.op("dve", lambda e: e.tensor_reduce(st4[:, 0:4], sq[:].rearrange("p (h v) -> p h v", v=128), AX.X, ALU.add), reads=["sq"], writes=["st4"])
            rstd_from_ss(st4[:, 0:4], 1.0 / 128.0, "st4")
            p.op("dve", lambda e: e.tensor_tensor(og[:], O3, st4[:, 0:4].unsqueeze(2).broadcast_to([128, 4, 128]), ALU.mult),
                 reads=[("SC", 1), "st4"], writes=["og"])
            p.op("pool", lambda e: e.tensor_tensor(obf[:, 512:1024], og[:].rearrange("p h v -> p (h v)"), sgo[:, i4, :], ALU.mult),
                 reads=["og", ("sgo", i4)], writes=["obf_gla"])
            if i4 == 1:
                dbg("og_%d_%d" % (s, tt), og[:], [128, 4, 128], ["og"])

        pb_state = {"n": 0}
        sc_state = {"n": 0}

        def nsa_group(s, tg, i4, g):
            tt = tg * 4 + i4
            gp0 = 64 * g
            qrhs = qTg[gp0:gp0 + 64, i4, :, :]

            def score_exp(kT_ap, kkeys):
                pi = pb_state["n"] % 3
                pb_state["n"] += 1
                sc_i = sc_state["n"] % 2
                sc_state["n"] += 1
                p.op("pe", lambda e: e.matmul(SC[:, sc_i, :].rearrange("p (h q) -> p h q", q=128), kT_ap, qrhs, start=True, stop=True),
                     reads=kkeys + [("qTg", i4)], writes=[("SC", sc_i)])
                p.op("act", lambda e: e.activation(Pb[pi][:], SC[:, sc_i, :].rearrange("p (h q) -> p h q", q=128), AF.Exp),
                     reads=[("SC", sc_i)], writes=[("Pb", pi)])
                return pi

            def pmask(pi, m_ap, mkeys):
                p.op("dve", lambda e: e.tensor_tensor(Pb[pi][:], Pb[pi][:], m_ap.unsqueeze(1).broadcast_to([128, 4, 128]), ALU.mult),
                     reads=[("Pb", pi)] + mkeys, writes=[("Pb", pi)])

            def pv(pi, v_ap, vkeys, col0, ncol, first, last):
                def f(e):
                    ins = None
                    for h in range(4):
                        ins = e.matmul(HB[:, h, col0:col0 + ncol], Pb[pi][:, h, :], v_ap, start=first, stop=last)
                    return ins
                p.op("pe", f, reads=[("Pb", pi)] + vkeys, writes=HK)

            def selection_chain():
                p.op("dve", lambda e: e.tensor_scalar(rinv[:, 0, :], HB[:, :, 194], 1e-30, None, ALU.max), reads=HK, writes=[("rinv", 0)])
                p.op("dve", lambda e: e.reciprocal(rinv[:, 0, :], rinv[:, 0, :]), reads=[("rinv", 0)], writes=[("rinv", 0)])
                for h in range(4):
                    p.op("dve", (lambda h: lambda e: e.scalar_tensor_tensor(score[:], HB[:, h, 195:227], rinv[:, 0, h:h + 1],
                                                                             selbias[:, tt, :] if h == 0 else score[:], ALU.mult, ALU.add))(h),
                         reads=HK + [("rinv", 0), "selbias", "score"], writes=["score"])
                p.op("dve", lambda e: e.max(top8[:], score[:]), reads=["score"], writes=["top8"])
                p.op("dve", lambda e: e.tensor_scalar(selm[:], score[:], top8[:, 7:8], None, ALU.is_ge), reads=["score", "top8"], writes=["selm"])
                th = transposes([selm[:]], ["selm"])
                p.op("act", lambda e: e.copy(selT[:], TR[0:32, th * 4, :]), reads=["TR"], writes=["selT"])

            def expansion(kt):
                nk = min(4, tt + 1 - kt)

                def f(e):
                    ins = None
                    for j in range(nk):
                        ins = e.matmul(MK[:, j * 128:(j + 1) * 128], emat[:, (kt + j) * 128:(kt + j + 1) * 128], selT[:], start=True, stop=True)
                    return ins
                p.op("pe", f, reads=["emat", "selT"], writes=["MK"])

            units = []
            units.append(dict(kT=kcmpT[gp0:gp0 + 64, :], kk=["kcmpT"], pre=None, masks=[(cmpmask[:, tt * 128:(tt + 1) * 128], ["cmpmask"])],
                              v=VcX[:, g, :], vk=["VcX"], col0=130, ncol=97, first=True, last=True, post=selection_chain))
            for kt in range(tt + 1):
                masks = [(MK[:, (kt % 4) * 128:(kt % 4 + 1) * 128], ["MK"])]
                if kt == tt:
                    masks.append((cm[:], ["cm"]))
                units.append(dict(kT=ksT[gp0:gp0 + 64, kt * 128:(kt + 1) * 128], kk=[("ksT", kt)],
                                  pre=((lambda kt: lambda: expansion(kt))(kt) if kt % 4 == 0 else None), masks=masks,
                                  v=vsx[:, kt, g, :], vk=[("vsx", kt)], col0=0, ncol=65, first=(kt == 0), last=(kt == tt), post=None))
            kts = [k_ for k_ in range(tt - 4, tt + 1) if k_ >= 0]
            for kt in kts:
                masks = []
                if kt == tt - 4:
                    masks.append((bm[:], ["bm"]))
                if kt == tt:
                    masks.append((cm[:], ["cm"]))
                units.append(dict(kT=kwT[gp0:gp0 + 64, kt * 128:(kt + 1) * 128], kk=[("kwT", kt)], pre=None, masks=masks,
                                  v=vwx[:, kt, g, :], vk=[("vwx", kt)], col0=65, ncol=65, first=(kt == kts[0]), last=(kt == tt), post=None))
            n_u = len(units)
            pis = [None] * n_u
            pis[0] = score_exp(units[0]["kT"], units[0]["kk"])
            for i, u in enumerate(units):
                if i + 1 < n_u:
                    pis[i + 1] = score_exp(units[i + 1]["kT"], units[i + 1]["kk"])
                if u["pre"] is not None:
                    u["pre"]()
                for (m_ap, mk) in u["masks"]:
                    pmask(pis[i], m_ap, mk)
                pv(pis[i], u["v"], u["vk"], u["col0"], u["ncol"], u["first"], u["last"])
                if u["post"] is not None:
                    u["post"]()
            p.op("dve", lambda e: e.reciprocal(rinv[:, 1, :], HB[:, :, 64]), reads=HK, writes=[("rinv", 1)])
            p.op("dve", lambda e: e.reciprocal(rinv[:, 2, :], HB[:, :, 129]), reads=HK, writes=[("rinv", 2)])
            sgv = sigg[:, i4, g * 12:(g + 1) * 12].rearrange("p (h b) -> p b h", b=3)
            p.op("dve", lambda e: e.tensor_tensor(gf[:], rinv[:], sgv, ALU.mult),
                 reads=[("rinv", 0), ("rinv", 1), ("rinv", 2), ("sigg", i4)], writes=["gf"])
            for b, col0 in ((0, 130), (1, 0), (2, 65)):
                dst = oacc if b == 0 else otmp
                p.op("dve", (lambda b, col0, dst: lambda e: e.tensor_tensor(dst[:], HB[:, :, col0:col0 + 64], gf[:, b, :].unsqueeze(2).broadcast_to([128, 4, 64]), ALU.mult))(b, col0, dst),
                     reads=HK + ["gf"], writes=["oacc" if b == 0 else "otmp"])
                if b == 1:
                    p.op("pool", lambda e: e.tensor_tensor(oacc[:], oacc[:], otmp[:], ALU.add), reads=["oacc", "otmp"], writes=["oacc"])
                if b == 2:
                    p.op("pool", lambda e: e.tensor_tensor(obf[:, g * 256:(g + 1) * 256].rearrange("p (h d) -> p h d", d=64), oacc[:], otmp[:], ALU.add),
                         reads=["oacc", "otmp"], writes=[("obf_nsa", g)])

        wo_state = {"n": 0}

        def out_proj(s, tg, i4):
            tt = tg * 4 + i4
            OB = [("obf_nsa", 0), ("obf_nsa", 1), "obf_gla"]
            for hh in range(2):
                th = transposes([obf[:, (hh * 4 + k) * 128:(hh * 4 + k + 1) * 128] for k in range(4)], OB)
                p.op("act", (lambda hh, th: lambda e: e.copy(oT[:, hh * 4:(hh + 1) * 4, :], TR[:, th * 4:(th + 1) * 4, :]))(hh, th),
                     reads=["TR"], writes=[("oT", hh)])
            if i4 == 1:
                dbg("obf_%d_%d" % (s, tt), obf[:], [128, DM], OB)
            xi = load_x(s, tt)
            for k in range(8):
                slot = wo_state["n"] % 2
                wo_state["n"] += 1
                p.op("sp", (lambda k, slot: lambda e: e.dma_start(out=woutc[slot][:], in_=wout_bf[k * 128:(k + 1) * 128, :]))(k, slot),
                     reads=[("wout_bf", k)], writes=[("woutc", slot)], dma=True)

                def f(e, k=k, slot=slot):
                    e.matmul(SC[:, 0, :], oT[:, k, :], woutc[slot][:, 0:512], start=(k == 0), stop=(k == 7))
                    return e.matmul(SC[:, 1, :], oT[:, k, :], woutc[slot][:, 512:1024], start=(k == 0), stop=(k == 7))
                p.op("pe", f, reads=[("oT", 0), ("oT", 1), ("woutc", slot)], writes=[("SC", 0), ("SC", 1)])
            for half in range(2):
                p.op("dve", (lambda half: lambda e: e.tensor_tensor(hres[:, i4, half * 512:(half + 1) * 512], SC[:, half, :], xt[xi][:, half * 512:(half + 1) * 512], ALU.add))(half),
                     reads=[("SC", half), ("xt", xi)], writes=[("hres", i4, half)])
            HR = [("hres", i4, 0), ("hres", i4, 1)]
            p.op("act", lambda e: e.activation(junk[:], hres[:, i4, :], AF.Square, accum_out=st4[:, 9:10]), reads=HR, writes=["junk", "st4h"])
            rstd_from_ss(st4[:, 9:10], 1.0 / DM, "st4h")
            p.op("dve", lambda e: e.scalar_tensor_tensor(hn[:], hres[:, i4, :], st4[:, 9:10], g2b[:], ALU.mult, ALU.mult),
                 reads=HR + ["st4h", "g2b"], writes=["hn"])
            for hh in range(2):
                th = transposes([hn[:, (hh * 4 + k) * 128:(hh * 4 + k + 1) * 128] for k in range(4)], ["hn"])
                p.op("act", (lambda hh, th: lambda e: e.copy(hnT[:, hh * 4:(hh + 1) * 4, i4 * 128:(i4 + 1) * 128], TR[:, th * 4:(th + 1) * 4, :]))(hh, th),
                     reads=["TR"], writes=[("hnT", i4)])

        WUP_KEYS = [("wup_bf", r) for r in range(8)]
        HN = [("hnT", i) for i in range(4)]

        def load_wup(fc, slot):
            def f(e):
                a = e.dma_start(out=wupc[slot][:, :, 0:128], in_=wup_bf[:, fc * 128:(fc + 1) * 128].rearrange("(k p) c -> p k c", p=128))
                b = e.dma_start(out=wupc[slot][:, :, 128:256], in_=wup_bf[:, DFF + fc * 128:DFF + (fc + 1) * 128].rearrange("(k p) c -> p k c", p=128))
                return [a, b]
            p.op("sp", f, reads=WUP_KEYS, writes=[("wupc", slot)], dma=True, ndma=2)

        def load_wdn(fc, slot):
            p.op("sp", lambda e: e.dma_start(out=wdnc[slot][:], in_=wdn_bf[fc * 128:(fc + 1) * 128, :]), reads=[("wdn_bf", fc)], writes=[("wdnc", slot)], dma=True)

        def ffn_up_chunk(fc, slot, gu):
            bank = slot * 2 + gu

            def f(e):
                ins = None
                for k in range(8):
                    ins = e.matmul(HB[:, bank, :], wupc[slot][:, k, gu * 128:(gu + 1) * 128], hnT[:, k, :], start=(k == 0), stop=(k == 7))
                return ins
            p.op("pe", f, reads=[("wupc", slot)] + HN, writes=[("HB", bank)])
            ub = ubuf[slot][gu]
            ukey = ("ub", slot, gu)
            cidx = gu * NFC + fc
            p.op("pool", lambda e: e.tensor_copy(ub[:, 0:2], halo[:, cidx, :]), reads=[("halo", cidx)], writes=[ukey])
            p.op("act", lambda e: e.copy(ub[:, 2:514], HB[:, bank, :]), reads=[("HB", bank)], writes=[ukey])
            p.op("pool", lambda e: e.tensor_copy(halo[:, cidx, :], ub[:, 512:514]), reads=[ukey], writes=[("halo", cidx)])
            yb = ybuf[gu]
            ykey = ("yb", gu)
            p.op("dve", lambda e: e.tensor_scalar(yb[:], ub[:, 2:514], convw[:, 2, cidx:cidx + 1], convb[:, cidx:cidx + 1], ALU.mult, ALU.add),
                 reads=[ukey, "convw", "convb"], writes=[ykey])
            p.op("dve", lambda e: e.scalar_tensor_tensor(yb[:], ub[:, 1:513], convw[:, 1, cidx:cidx + 1], yb[:], ALU.mult, ALU.add),
                 reads=[ukey, ykey, "convw"], writes=[ykey])
            p.op("dve", lambda e: e.scalar_tensor_tensor(yb[:], ub[:, 0:512], convw[:, 0, cidx:cidx + 1], yb[:], ALU.mult, ALU.add),
                 reads=[ukey, ykey, "convw"], writes=[ykey])

        def ffn_act(fc):
            p.op("act", lambda e: e.activation(ybuf[0][:], ybuf[0][:], AF.Silu), reads=[("yb", 0)], writes=[("yb", 0)])
            p.op("dve", lambda e: e.tensor_tensor(actT[:, fc, :], ybuf[0][:], ybuf[1][:], ALU.mult), reads=[("yb", 0), ("yb", 1)], writes=[("actT", fc)])

        def ffn_down(s, tg, pair):
            load_wdn(0, 0)
            for fc in range(NFC):
                slot = fc % 2
                if fc + 1 < NFC:
                    load_wdn(fc + 1, (fc + 1) % 2)

                def f(e, fc=fc, slot=slot):
                    ins = None
                    for t2 in range(2):
                        i4 = pair * 2 + t2
                        for half in range(2):
                            ins = e.matmul(HB[:, t2 * 2 + half, :], actT[:, fc, i4 * 128:(i4 + 1) * 128], wdnc[slot][:, half * 512:(half + 1) * 512],
                                           start=(fc == 0), stop=(fc == NFC - 1))
                    return ins
                p.op("pe", f, reads=[("actT", fc), ("wdnc", slot)], writes=HK)
            for t2 in range(2):
                i4 = pair * 2 + t2
                tt = tg * 4 + i4
                HR = [("hres", i4, 0), ("hres", i4, 1)]
                p.op("dve", (lambda t2, i4: lambda e: e.tensor_tensor(hres[:, i4, :].rearrange("p (a c) -> p a c", a=2), HB[:, t2 * 2:t2 * 2 + 2, :],
                                                                      hres[:, i4, :].rearrange("p (a c) -> p a c", a=2), ALU.add))(t2, i4),
                     reads=HK + HR, writes=HR)
                r0 = s * S + tt * 128
                p.op("sp", (lambda i4, r0: lambda e: e.dma_start(out=out[r0:r0 + 128, :], in_=hres[:, i4, :]))(i4, r0), reads=HR, dma=True)

        REGION_KEYS = ([("wing", i) for i in range(2)] + [("xnT", i, h) for i in range(4) for h in range(2)]
                       + [("actT", fc) for fc in range(NFC)]
                       + [("eB", i, j) for i in range(4) for j in range(3)] + [("sgo", i) for i in range(4)] + [("gvb", i) for i in range(4)]
                       + [("ub", i, j) for i in range(2) for j in range(2)] + [("yb", j) for j in range(2)]
                       + [("wupc", i) for i in range(2)] + [("wdnc", i) for i in range(2)])

        def region_barrier():
            p.op("pool", lambda e: e.memset(dummy[:], 0.0), writes=REGION_KEYS)

        def do_g4(s, tg):
            region_barrier()
            load_wup(0, 0)
            for fc in range(NFC):
                slot = fc % 2
                if fc + 1 < NFC:
                    load_wup(fc + 1, (fc + 1) % 2)
                for gu in range(2):
                    ffn_up_chunk(fc, slot, gu)
                ffn_act(fc)
            if tg == 0:
                dbg("actT_%d" % s, actT, [128, NFC, 512], [("actT", fc) for fc in range(NFC)])
            for pair in range(2):
                ffn_down(s, tg, pair)
            region_barrier()

        for s in range(nseq):
            if stop < 1:
                break
            import os
            skip = os.environ.get("SKIPMS", "")
            if "0" not in skip: p.op("pool", lambda e: e.memset(kcTg[:, 0:16], 0.0), writes=["kcTg"])
            if "1" not in skip: p.op("pool", lambda e: e.memset(vcTg[:, 0:16], 0.0), writes=["vcTg"])
            if "2" not in skip: p.op("pool", lambda e: e.memset(kcmpT[:], 0.0), writes=["kcmpT"])
            if "3" not in skip: p.op("pool", lambda e: e.memset(vcmpT[:], 0.0), writes=["vcmpT"])
            if "4" not in skip: p.op("pool", lambda e: e.memset(Sst[:], 0.0), writes=["Sst"])
            if "5" not in skip: p.op("pool", lambda e: e.memset(S2_bf[:], 0.0), writes=["S2_lo", "S2_hi"])
            if "6" not in skip: p.op("pool", lambda e: e.memset(halo[:], 0.0), writes=[("halo", i) for i in range(2 * NFC)])
            for tg in range(4):
                if stop >= 2:
                    do_g1(s, tg)
                if stop >= 3:
                    do_g2(s, tg)
                for i4 in range(4):
                    if stop >= 4:
                        gla_core(s, tg, i4)
                    if stop >= 5:
                        for g in range(2):
                            nsa_group(s, tg, i4, g)
                    if stop >= 6:
                        out_proj(s, tg, i4)
                if tg == 0 and stop >= 6:
                    dbg("hres_%d" % s, hres[:], [128, 4, DM], [("hres", i, h_) for i in range(4) for h_ in range(2)])
                if stop >= 7:
                    do_g4(s, tg)
                if stop < 99:
                    break

        p.wait_all("sp", [o for e_ in ENG_NAMES for o in p.ops[e_] if o.dma])
        p.emit(nc, st)
    return nc, dbg_out


_CACHE = {}


def run_cores(inputs, nseq, dbg_names=(), stop=99):
    key = (nseq, tuple(dbg_names), stop)
    if key not in _CACHE:
        _CACHE[key] = build_program(nseq, dbg_names, stop)
    nc, dbg_out = _CACHE[key]
    consts = make_consts()
    x = np.ascontiguousarray(inputs["x"], dtype=np.float32)
    in_maps = []
    for c in range(NCORES):
        m = {"x": np.ascontiguousarray(x[c * nseq:(c + 1) * nseq].reshape(nseq * S, DM))}
        for k, shp in WEIGHT_SHAPES.items():
            m[k] = np.ascontiguousarray(np.asarray(inputs[k], dtype=np.float32).reshape(shp))
        for k in CONST_SHAPES:
            m["c_" + k] = consts[k]
        in_maps.append(m)
    res = run_bass_kernel_spmd(nc, in_maps, core_ids=list(range(NCORES)))
    return res


def kernel(**inputs):
    nseq = inputs["x"].shape[0] // NCORES
    res = run_cores(inputs, nseq)
    outs = [np.asarray(r["out"], dtype=np.float32).reshape(nseq, S, DM) for r in res.results]
    return np.concatenate(outs, axis=0)
```
